# Optimizing a Trainium2 kernel written in Bass

```python
import math
import jax, jax.numpy as jnp
from jax import lax
import numpy as np

D_MODEL = 1024
BATCH = 4
SEQ = 4096
DEPTH = 2

N_MEM = 256
Q_BLOCK = 128
EPS = 1e-6
LRU_WIDTH = D_MODEL // 2
LRU_BLOCKS = 8
LRU_BLOCK_DIM = LRU_WIDTH // LRU_BLOCKS
CONV_WIDTH = 4
LRU_C = 8.0
FOX_HEADS = 8
FOX_HEAD_DIM = (D_MODEL // 2) // FOX_HEADS
FOX_WIDTH = FOX_HEADS * FOX_HEAD_DIM
EVEN_SPLITS = [LRU_WIDTH, LRU_WIDTH, FOX_WIDTH, FOX_WIDTH, FOX_WIDTH, FOX_HEADS]
EVEN_IN = sum(EVEN_SPLITS)
EVEN_MIX = LRU_WIDTH + FOX_WIDTH
DIFF_HEADS = 8
DIFF_HEAD_DIM = D_MODEL // (2 * DIFF_HEADS)
DIFF_V_DIM = 2 * DIFF_HEAD_DIM
DIFF_WIDTH = DIFF_HEADS * DIFF_V_DIM
ODD_IN = 3 * DIFF_WIDTH
XATTN_HEADS = 4
XATTN_HEAD_DIM = D_MODEL // XATTN_HEADS
D_FF = 4 * D_MODEL
N_EVEN = (DEPTH + 1) // 2
N_ODD = DEPTH // 2

kernel_name = "hybrid_rglru_fox_diffattn_trunk"


def _rmsnorm(x, g):
    xf = x.astype(jnp.float32)
    y = xf * lax.rsqrt(jnp.mean(xf * xf, axis=-1, keepdims=True) + EPS)
    return (y * g.astype(jnp.float32)).astype(x.dtype)


def _sweep_query_blocks(block_fn, seq_len):
    starts = jnp.arange(seq_len // Q_BLOCK) * Q_BLOCK
    out = lax.map(block_fn, starts)
    n, b, h, qb, dv = out.shape
    return out.transpose(1, 2, 0, 3, 4).reshape(b, h, n * qb, dv)


def _rglru_group(xb, gb, conv_w, conv_b, w_r, b_r, w_i, b_i, lam):
    b, s, w = xb.shape
    xp = jnp.pad(xb, ((0, 0), (CONV_WIDTH - 1, 0), (0, 0)))
    xc = conv_b + sum(xp[:, k:k + s] * conv_w[k] for k in range(CONV_WIDTH))
    xg = xc.reshape(b, s, LRU_BLOCKS, LRU_BLOCK_DIM)
    r = jax.nn.sigmoid(jnp.einsum('bsgi,gij->bsgj', xg, w_r).reshape(b, s, w) + b_r)
    i = jax.nn.sigmoid(jnp.einsum('bsgi,gij->bsgj', xg, w_i).reshape(b, s, w) + b_i)
    log_a = LRU_C * r.astype(jnp.float32) * jax.nn.log_sigmoid(lam.astype(jnp.float32))
    a = jnp.exp(log_a)
    u = jnp.sqrt(-jnp.expm1(2.0 * log_a)) * (i * xc).astype(jnp.float32)

    def combine(left, right):
        a1, b1 = left
        a2, b2 = right
        return a1 * a2, a2 * b1 + b2

    _, h = lax.associative_scan(combine, (a, u), axis=1)
    return h.astype(xb.dtype) * jax.nn.gelu(gb)


def _fox_attention(q, k, v, log_f):
    b, s, h, dh = q.shape
    qf = (q.astype(jnp.float32) * dh ** -0.5).transpose(0, 2, 1, 3)
    kf = k.astype(jnp.float32).transpose(0, 2, 1, 3)
    vf = v.astype(jnp.float32).transpose(0, 2, 1, 3)
    cum = jnp.cumsum(log_f.astype(jnp.float32), axis=1).transpose(0, 2, 1)
    k_pos = jnp.arange(s)

    def block(start):
        qb = lax.dynamic_slice_in_dim(qf, start, Q_BLOCK, axis=2)
        cq = lax.dynamic_slice_in_dim(cum, start, Q_BLOCK, axis=2)
        q_pos = start + jnp.arange(Q_BLOCK)
        logits = jnp.einsum('bhqd,bhkd->bhqk', qb, kf) + cq[..., :, None] - cum[..., None, :]
        logits = jnp.where(k_pos[None, :] <= q_pos[:, None], logits, -jnp.inf)
        p = jax.nn.softmax(logits, axis=-1)
        return jnp.einsum('bhqk,bhkd->bhqd', p, vf)

    o = _sweep_query_blocks(block, s)
    return o.transpose(0, 2, 1, 3).reshape(b, s, h * dh).astype(q.dtype)


def _diff_attention(q, k, v, lam, lambda_init, norm_g):
    b, s, h, _, dh = q.shape
    dv = v.shape[-1]
    qf = (q.astype(jnp.float32) * dh ** -0.5).transpose(0, 2, 3, 1, 4)
    kf = k.astype(jnp.float32).transpose(0, 2, 3, 1, 4)
    vf = v.astype(jnp.float32).transpose(0, 2, 1, 3)
    k_pos = jnp.arange(s)

    def block(start):
        qb = lax.dynamic_slice_in_dim(qf, start, Q_BLOCK, axis=3)
        q_pos = start + jnp.arange(Q_BLOCK)
        logits = jnp.einsum('bhcqd,bhckd->bhcqk', qb, kf)
        logits = jnp.where(k_pos[None, :] <= q_pos[:, None], logits, -jnp.inf)
        p = jax.nn.softmax(logits, axis=-1)
        w = p[:, :, 0] - lam * p[:, :, 1]
        return jnp.einsum('bhqk,bhkd->bhqd', w, vf)

    o = _sweep_query_blocks(block, s)
    o = o * lax.rsqrt(jnp.mean(o * o, axis=-1, keepdims=True) + EPS)
    o = o * norm_g.astype(jnp.float32) * (1.0 - lambda_init)
    return o.transpose(0, 2, 1, 3).reshape(b, s, h * dv).astype(q.dtype)


def _memory_cross_attention(xn, mem_n, wq, wkv, wo):
    b, s, d = xn.shape
    m = mem_n.shape[1]
    q = (xn @ wq).reshape(b, s, XATTN_HEADS, XATTN_HEAD_DIM).astype(jnp.float32)
    kv = (mem_n @ wkv).reshape(b, m, 2, XATTN_HEADS, XATTN_HEAD_DIM).astype(jnp.float32)
    logits = jnp.einsum('bshd,bmhd->bhsm', q, kv[:, :, 0]) * XATTN_HEAD_DIM ** -0.5
    p = jax.nn.softmax(logits, axis=-1)
    o = jnp.einsum('bhsm,bmhd->bshd', p, kv[:, :, 1]).reshape(b, s, d).astype(xn.dtype)
    return o @ wo


def _sq_relu_mlp(xn, w_up, w_down):
    return jnp.square(jax.nn.relu(xn @ w_up)) @ w_down


def setup_inputs(seed: int = 0) -> dict:
    key = jax.random.key(seed)
    ks = iter(jax.random.split(key, 40))

    def nrm(shape, fan_in):
        return jax.random.normal(next(ks), shape, jnp.float32) * fan_in ** -0.5

    def gain(shape):
        return 1.0 + 0.02 * jax.random.normal(next(ks), shape, jnp.float32)

    def small(shape, scale=0.02):
        return scale * jax.random.normal(next(ks), shape, jnp.float32)

    u = jax.random.uniform(next(ks), (N_EVEN, LRU_WIDTH), jnp.float32, 0.9, 0.999)
    return {
        "x": jax.random.normal(next(ks), (BATCH, SEQ, D_MODEL), jnp.float32),
        "mem": jax.random.normal(next(ks), (BATCH, N_MEM, D_MODEL), jnp.float32),
        "g_mem": gain((D_MODEL,)),
        "g_final": gain((D_MODEL,)),
        "mix_norm_g": gain((DEPTH, D_MODEL)),
        "xattn_norm_g": gain((DEPTH, D_MODEL)),
        "mlp_norm_g": gain((DEPTH, D_MODEL)),
        "w_in_even": nrm((N_EVEN, D_MODEL, EVEN_IN), D_MODEL),
        "conv_w": nrm((N_EVEN, CONV_WIDTH, LRU_WIDTH), CONV_WIDTH),
        "conv_b": small((N_EVEN, LRU_WIDTH)),
        "w_rgate": nrm((N_EVEN, LRU_BLOCKS, LRU_BLOCK_DIM, LRU_BLOCK_DIM), LRU_BLOCK_DIM),
        "b_rgate": small((N_EVEN, LRU_WIDTH)),
        "w_igate": nrm((N_EVEN, LRU_BLOCKS, LRU_BLOCK_DIM, LRU_BLOCK_DIM), LRU_BLOCK_DIM),
        "b_igate": small((N_EVEN, LRU_WIDTH)),
        "lru_lambda": jnp.log(u) - jnp.log1p(-u),
        "fox_forget_b": 3.0 + small((N_EVEN, FOX_HEADS), 0.1),
        "w_out_even": nrm((N_EVEN, EVEN_MIX, D_MODEL), EVEN_MIX),
        "w_in_odd": nrm((N_ODD, D_MODEL, ODD_IN), D_MODEL),
        "lambda_q1": small((N_ODD, DIFF_HEAD_DIM), 0.1),
        "lambda_k1": small((N_ODD, DIFF_HEAD_DIM), 0.1),
        "lambda_q2": small((N_ODD, DIFF_HEAD_DIM), 0.1),
        "lambda_k2": small((N_ODD, DIFF_HEAD_DIM), 0.1),
        "diff_norm_g": gain((N_ODD, DIFF_V_DIM)),
        "w_out_odd": nrm((N_ODD, DIFF_WIDTH, D_MODEL), DIFF_WIDTH),
        "xattn_wq": nrm((DEPTH, D_MODEL, D_MODEL), D_MODEL),
        "xattn_wkv": nrm((DEPTH, D_MODEL, 2 * D_MODEL), D_MODEL),
        "xattn_wo": nrm((DEPTH, D_MODEL, D_MODEL), D_MODEL),
        "w_up": nrm((DEPTH, D_MODEL, D_FF), D_MODEL),
        "w_down": nrm((DEPTH, D_FF, D_MODEL), D_FF),
    }


def reference(x, mem, g_mem, g_final, mix_norm_g, xattn_norm_g, mlp_norm_g,
              w_in_even, conv_w, conv_b, w_rgate, b_rgate, w_igate, b_igate,
              lru_lambda, fox_forget_b, w_out_even,
              w_in_odd, lambda_q1, lambda_k1, lambda_q2, lambda_k2, diff_norm_g, w_out_odd,
              xattn_wq, xattn_wkv, xattn_wo, w_up, w_down):
    b, s, d = x.shape
    split_at = [int(c) for c in np.cumsum(EVEN_SPLITS)[:-1]]
    mem_n = _rmsnorm(mem, g_mem)
    h = x
    for layer in range(DEPTH):
        i = layer // 2
        hn = _rmsnorm(h, mix_norm_g[layer])
        if layer % 2 == 0:
            z = hn @ w_in_even[i]
            xb, gb, q, k, v, f_logit = jnp.split(z, split_at, axis=-1)
            y_lru = _rglru_group(xb, gb, conv_w[i], conv_b[i], w_rgate[i], b_rgate[i],
                                 w_igate[i], b_igate[i], lru_lambda[i])
            log_f = jax.nn.log_sigmoid((f_logit + fox_forget_b[i]).astype(jnp.float32))
            y_fox = _fox_attention(q.reshape(b, s, FOX_HEADS, FOX_HEAD_DIM),
                                   k.reshape(b, s, FOX_HEADS, FOX_HEAD_DIM),
                                   v.reshape(b, s, FOX_HEADS, FOX_HEAD_DIM), log_f)
            mixed = jnp.concatenate([y_lru, y_fox], axis=-1) @ w_out_even[i]
        else:
            z = hn @ w_in_odd[i]
            q, k, v = jnp.split(z, 3, axis=-1)
            lambda_init = 0.8 - 0.6 * math.exp(-0.3 * layer)
            f32 = jnp.float32
            lam = (jnp.exp(jnp.sum(lambda_q1[i].astype(f32) * lambda_k1[i].astype(f32)))
                   - jnp.exp(jnp.sum(lambda_q2[i].astype(f32) * lambda_k2[i].astype(f32)))
                   + lambda_init)
            y = _diff_attention(q.reshape(b, s, DIFF_HEADS, 2, DIFF_HEAD_DIM),
                                k.reshape(b, s, DIFF_HEADS, 2, DIFF_HEAD_DIM),
                                v.reshape(b, s, DIFF_HEADS, DIFF_V_DIM),
                                lam, lambda_init, diff_norm_g[i])
            mixed = y @ w_out_odd[i]
        h = h + mixed
        h = h + _memory_cross_attention(_rmsnorm(h, xattn_norm_g[layer]), mem_n,
                                        xattn_wq[layer], xattn_wkv[layer], xattn_wo[layer])
        h = h + _sq_relu_mlp(_rmsnorm(h, mlp_norm_g[layer]), w_up[layer], w_down[layer])
    return _rmsnorm(h, g_final)
```

```python
import math
from contextlib import ExitStack

import numpy as np
import concourse.bass as bass
import concourse.mybir as mybir
from concourse.bass_utils import run_bass_kernel_spmd

F32 = mybir.dt.float32
BF16 = mybir.dt.bfloat16
AF = mybir.ActivationFunctionType
ALU = mybir.AluOpType
AX = mybir.AxisListType

D = 1024
S = 4096
NB = 4
NMEM = 256
EPS = 1e-6
DEPTH = 2
EVEN_IN = 2568
DFF = 4096
TT = 512
NTT = S // TT
NBLK = S // 128
N_CORES = 8
CORE_OF_BATCH = [0, 1, 4, 5]


class Buf:
    __slots__ = ("name", "w", "r", "strict")

    def __init__(self, name="", strict=False):
        self.name = name
        self.strict = strict
        self.w = []
        self.r = {}


class Op:
    __slots__ = ("eng", "fn", "reads", "writes", "dma", "need_inc", "ev", "phase", "waits", "sem_prev", "wadd")

    def __init__(self, eng, fn, reads, writes, dma, wadd=()):
        self.wadd = list(wadd)
        self.eng = eng
        self.fn = fn
        self.reads = reads
        self.writes = writes
        self.dma = dma
        self.need_inc = dma
        self.ev = None
        self.waits = None
        self.sem_prev = None


class Prog:
    ENG = ("pe", "act", "dve", "pool", "sp")
    EPOCH = 8000
    NDMA = 40

    def __init__(self, nc):
        self.nc = nc
        self.eo = {"pe": nc.tensor, "act": nc.scalar, "dve": nc.vector, "pool": nc.gpsimd, "sp": nc.sync}
        self.ops = []
        self.phase = 0
        self.stack = ExitStack()
        self.cnt = {e: 0 for e in self.ENG}
        self.esems = {e: [] for e in self.ENG}
        self.dsems = []
        self.dtot = [0] * self.NDMA
        self.dlast = [None] * self.NDMA
        self.dma_i = 0
        self.last = {e: None for e in self.ENG}
        self.waited = {e: {} for e in self.ENG}
        self.n_ins = 0

    def op(self, eng, fn, reads=(), writes=()):
        o = Op(eng, fn, list(reads), list(writes), False)
        self.ops.append(o)
        return o

    def dma(self, q, out, in_, reads=(), writes=(), wadd=()):
        o = Op(q, lambda e: e.dma_start(out=out, in_=in_), list(reads), list(writes), True, wadd)
        self.ops.append(o)
        return o

    def _esem(self, e, k):
        lst = self.esems[e]
        while len(lst) <= k:
            lst.append(self.stack.enter_context(self.nc.semaphore("s_%s_%d" % (e, len(lst)))))
        return lst[k]

    def _dsem(self, i):
        while len(self.dsems) <= i:
            self.dsems.append(self.stack.enter_context(self.nc.semaphore("s_dma_%d" % len(self.dsems))))
        return self.dsems[i]

    def flush(self):
        ops = self.ops
        self.ops = []
        ph = self.phase
        for o in ops:
            o.phase = ph
        for o in ops:
            deps = []
            for b in o.reads:
                for w in b.w:
                    if w.phase == ph:
                        deps.append((w, True))
            for b in o.writes:
                for w in b.w:
                    if w.phase == ph:
                        deps.append((w, b.strict))
                for r in b.r.values():
                    if r.phase == ph:
                        deps.append((r, False))
            for b in o.wadd:
                for r in b.r.values():
                    if r.phase == ph:
                        deps.append((r, False))
            real = []
            for d, raw in deps:
                if d is o:
                    continue
                if (not d.dma) and (not o.dma) and d.eng == o.eng and not raw:
                    continue
                real.append(d)
                d.need_inc = True
            o.waits = real
            for b in o.reads:
                key = ("dma", id(o)) if o.dma else o.eng
                b.r[key] = o
            for b in o.writes:
                b.w = [o]
                b.r = {}
            for b in o.wadd:
                b.w.append(o)
            if not o.dma:
                self.last[o.eng] = o
        for e in self.ENG:
            if self.last[e] is not None and self.last[e].phase == ph:
                self.last[e].need_inc = True
        for o in ops:
            if o.dma:
                s = self.dma_i % self.NDMA
                self.dma_i += 1
                o.sem_prev = self.dlast[s]
                self.dtot[s] += 16
                self.dlast[s] = o
                o.ev = (self._dsem(s), self.dtot[s])
            elif o.need_inc:
                self.cnt[o.eng] += 1
                c = self.cnt[o.eng] - 1
                o.ev = (self._esem(o.eng, c // self.EPOCH), c % self.EPOCH + 1)
        for o in ops:
            e = self.eo[o.eng]
            wt = self.waited[o.eng]
            lst = list(o.waits)
            if o.dma and o.sem_prev is not None:
                lst.append(o.sem_prev)
            for d in lst:
                sem, val = d.ev
                if wt.get(id(sem), 0) >= val:
                    continue
                e.wait_ge(sem, val)
                wt[id(sem)] = val
                self.n_ins += 1
            ins = o.fn(e)
            self.n_ins += 1
            if o.dma:
                ins.then_inc(o.ev[0], 16)
            elif o.need_inc:
                ins.then_inc(o.ev[0], 1)
        for e in self.ENG:
            eo = self.eo[e]
            wt = self.waited[e]
            for f in self.ENG:
                lo = self.last[f]
                if f == e or lo is None or lo.ev is None:
                    continue
                sem, val = lo.ev
                if wt.get(id(sem), 0) < val:
                    eo.wait_ge(sem, val)
                    wt[id(sem)] = val
            for s in range(len(self.dsems)):
                sem = self.dsems[s]
                if wt.get(id(sem), 0) < self.dtot[s]:
                    eo.wait_ge(sem, self.dtot[s])
                    wt[id(sem)] = self.dtot[s]
        self.phase += 1

    def close(self):
        self.stack.close()


class _Ticker:
    def __init__(self, p, gen, every):
        self.p, self.gen, self.every, self.n = p, gen, every, 0

    def _tick(self):
        self.n += 1
        if self.gen is not None and self.n % self.every == 0:
            if next(self.gen, "done") == "done":
                self.gen = None

    def op(self, *a, **k):
        r = self.p.op(*a, **k)
        self._tick()
        return r

    def dma(self, *a, **k):
        r = self.p.dma(*a, **k)
        self._tick()
        return r

    def drain(self):
        if self.gen is not None:
            for _ in self.gen:
                pass
            self.gen = None

    def flush(self):
        self.p.flush()


class Builder:
    def __init__(self, debug=(), stop=None):
        self.nc = bass.Bass("TRN2", target_bir_lowering=False)
        self.p = Prog(self.nc)
        self.debug = set(debug)
        self.stop = stop
        self.gst = ExitStack()
        self.uid = 0
        self.cast_i = 0

    def dram_in(self, name, shape, dt=F32):
        return self.nc.dram_tensor(name, list(shape), dt, kind="ExternalInput").ap()

    def dram_tmp(self, name, shape, dt):
        kind = "ExternalOutput" if name in self.debug else "Internal"
        return self.nc.dram_tensor(name, list(shape), dt, kind=kind).ap()

    def sb(self, st, name, shape, dt):
        self.uid += 1
        return st.enter_context(self.nc.sbuf_tensor("%s_%d" % (name, self.uid), list(shape), dt))

    def ps(self, st, name, shape, dt=F32):
        self.uid += 1
        return st.enter_context(self.nc.psum_tensor("%s_%d" % (name, self.uid), list(shape), dt))

    def load_w(self, *a, **k):
        for _ in self.load_w_gen(*a, **k):
            pass

    def load_w_gen(self, st, dst, dstbuf, src, KC, N, gcol=None, gbuf=None, mul2=1.0, ncol_chunk=2048, engs=None):
        p = self.p
        dstbufs = dstbuf if isinstance(dstbuf, list) else [dstbuf]
        if not hasattr(self, "_stg") or self._stg_st is not st:
            self._stg = [self.sb(st, "stg", [128, 2048], F32) for _ in range(3)]
            self._stgb = [Buf("stg%d" % i) for i in range(3)]
            self._stg_st = st
        for kc in range(KC):
            for c0 in range(0, N, ncol_chunk):
                cw = min(ncol_chunk, N - c0)
                i = self.cast_i % len(self._stg)
                self.cast_i += 1
                stg, sbf = self._stg[i], self._stgb[i]
                p.dma("sp", stg[:, 0:cw], src[kc * 128:(kc + 1) * 128, c0:c0 + cw], writes=[sbf])
                o = dst[:, kc, c0:c0 + cw]
                eng = (engs or ("pool", "dve", "act", "dve"))[self.cast_i % len(engs or (0, 1, 2, 3))]
                if gcol is not None:
                    g = gcol[:, kc:kc + 1]
                    if eng == "act":
                        if mul2 != 1.0:
                            eng = "dve"
                        else:
                            p.op("act", lambda e, o=o, s=stg[:, 0:cw], g=g: e.activation(
                                out=o, in_=s, func=AF.Copy, scale=g), reads=[sbf, gbuf], writes=dstbufs)
                    if eng != "act":
                        p.op(eng, lambda e, o=o, s=stg[:, 0:cw], g=g: e.tensor_scalar(
                            out=o, in0=s, scalar1=g, scalar2=float(mul2), op0=ALU.mult, op1=ALU.mult),
                            reads=[sbf, gbuf], writes=dstbufs)
                elif eng == "act":
                    p.op("act", lambda e, o=o, s=stg[:, 0:cw]: e.copy(out=o, in_=s), reads=[sbf], writes=dstbufs)
                else:
                    p.op(eng, lambda e, o=o, s=stg[:, 0:cw]: e.tensor_copy(out=o, in_=s),
                         reads=[sbf], writes=dstbufs)
                yield

    def load_col(self, st, name, src, KC):
        t = self.sb(st, name, [128, KC], F32)
        b = Buf(name)
        for kc in range(KC):
            self.p.dma("sp", t[:, kc:kc + 1], src[kc * 128:(kc + 1) * 128].rearrange("(p o) -> p o", o=1),
                       wadd=[b])
        return t, b

    def rms_cast(self, x, xb, xn, xnb, J, junk, junkb, ss, ssb, rstd, rstdb):
        p = self.p
        for j in range(J):
            p.op("act", lambda e, j=j: e.activation(out=junk[:, :], in_=x[:, j, :], func=AF.Square,
                                                    accum_out=ss[:, j:j + 1]),
                 reads=[xb], writes=[junkb, ssb])
        p.op("dve", lambda e: e.tensor_scalar(out=rstd[:, 0:J], in0=ss[:, 0:J], scalar1=1.0 / D, scalar2=EPS,
                                              op0=ALU.mult, op1=ALU.add), reads=[ssb], writes=[rstdb])
        p.op("act", lambda e: e.activation(out=rstd[:, 0:J], in_=rstd[:, 0:J], func=AF.Sqrt),
             reads=[rstdb], writes=[rstdb])
        p.op("dve", lambda e: e.reciprocal(out=rstd[:, 0:J], in_=rstd[:, 0:J]), reads=[rstdb], writes=[rstdb])
        for j in range(J):
            p.op("dve", lambda e, j=j: e.tensor_scalar(out=xn[:, j, :], in0=x[:, j, :], scalar1=rstd[:, j:j + 1],
                                                       scalar2=None, op0=ALU.mult),
                 reads=[xb, rstdb], writes=[xnb])

    def transpose_to(self, src, srcb, dst, dstb, J, C, pst, pstb, ident, identb, evac_eng=("act", "dve")):
        p = self.p
        for j in range(J):
            k = j % len(pst)
            pt, ptb = pst[k], pstb[k]
            for c in range(C):
                p.op("pe", lambda e, j=j, c=c, pt=pt: e.transpose(out=pt[:, c, :], in_=src[:, j, c * 128:(c + 1) * 128],
                                                                 identity=ident[:, :]),
                     reads=[srcb, identb], writes=[ptb])
            eng = evac_eng[j % len(evac_eng)]
            if eng == "act":
                p.op("act", lambda e, j=j, pt=pt: e.copy(out=dst[:, 0:C, j * 128:(j + 1) * 128], in_=pt[:, 0:C, :]),
                     reads=[ptb], writes=[dstb])
            else:
                p.op(eng, lambda e, j=j, pt=pt: e.tensor_copy(out=dst[:, 0:C, j * 128:(j + 1) * 128], in_=pt[:, 0:C, :]),
                     reads=[ptb], writes=[dstb])

    def build(self):
        nc, p = self.nc, self.p
        I = {}
        I["x"] = self.dram_in("x", [S, D])
        I["mem"] = self.dram_in("mem", [NMEM, D])
        for n in ("g_mem", "g_final"):
            I[n] = self.dram_in(n, [D])
        for n in ("mix_norm_g", "xattn_norm_g", "mlp_norm_g"):
            I[n] = self.dram_in(n, [DEPTH, D])
        I["w_in_even"] = self.dram_in("w_in_even", [D, EVEN_IN])
        I["conv_w"] = self.dram_in("conv_w", [4, 512])
        for n in ("conv_b", "b_rgate", "b_igate", "lru_lambda"):
            I[n] = self.dram_in(n, [512])
        I["w_rgate"] = self.dram_in("w_rgate", [8, 64, 64])
        I["w_igate"] = self.dram_in("w_igate", [8, 64, 64])
        I["fox_forget_b"] = self.dram_in("fox_forget_b", [8])
        I["w_out_even"] = self.dram_in("w_out_even", [D, D])
        I["w_in_odd"] = self.dram_in("w_in_odd", [D, 3 * D])
        for n in ("lambda_q1", "lambda_k1", "lambda_q2", "lambda_k2"):
            I[n] = self.dram_in(n, [64])
        I["diff_norm_g"] = self.dram_in("diff_norm_g", [128])
        I["w_out_odd"] = self.dram_in("w_out_odd", [D, D])
        I["xattn_wq"] = self.dram_in("xattn_wq", [DEPTH, D, D])
        I["xattn_wkv"] = self.dram_in("xattn_wkv", [DEPTH, D, 2 * D])
        I["xattn_wo"] = self.dram_in("xattn_wo", [DEPTH, D, D])
        I["w_up"] = self.dram_in("w_up", [DEPTH, D, DFF])
        I["w_down"] = self.dram_in("w_down", [DEPTH, DFF, D])
        self.I = I
        self.out = nc.dram_tensor("out", [S, D], F32, kind="ExternalOutput").ap()

        T = {}
        T["h"] = self.dram_tmp("h", [S, D], F32)
        T["xgT"] = self.dram_tmp("xgT", [1024, S], F32)
        T["qT"] = self.dram_tmp("qT", [1024, S], BF16)
        T["kT"] = self.dram_tmp("kT", [1024, S], BF16)
        T["fT"] = self.dram_tmp("fT", [8, S], F32)
        T["vE0"] = self.dram_tmp("vE0", [S, 8, 65], BF16)
        T["vE1"] = self.dram_tmp("vE1", [S, 8, 129], BF16)
        T["ylruT"] = self.dram_tmp("ylruT", [512, S], BF16)
        T["y"] = self.dram_tmp("y", [S, D], BF16)
        T["xnT"] = self.dram_tmp("xnT", [D, S], BF16)
        self.T = T
        self.Tb = {k: Buf(k) for k in T}

        g = self.gst
        self.ident = self.sb(g, "ident", [128, 128], BF16)
        self.identb = Buf("ident")
        self.identf = self.sb(g, "identf", [128, 128], F32)
        self.identfb = Buf("identf")
        self.tri = self.sb(g, "tri", [128, 128], BF16)
        self.trib = Buf("tri")
        self.sel = self.sb(g, "sel", [128, 128], F32)
        self.selb = Buf("sel")
        self.kmemT = [self.sb(g, "kmemT", [128, 8, NMEM], BF16) for _ in range(DEPTH)]
        self.kmemTb = [Buf("kmemT") for _ in range(DEPTH)]
        self.vmemE = [self.sb(g, "vmemE", [128, 2, 4, 257], BF16) for _ in range(DEPTH)]
        self.vmemEb = [Buf("vmemE") for _ in range(DEPTH)]
        self.consts()
        self.fuse_mem = (self.stop != "mem")
        if not self.fuse_mem:
            self.phase_mem()
            return self.finish()
        self.WA = None
        self.win_pre = None
        self._nxt = None
        for layer in range(DEPTH):
            self.phase_inproj(layer)
            if self._nxt is not None:
                self._nxt.close()
                self._nxt = None
                self.WA = None
                self.win_pre = None
            if self.stop == "inproj%d" % layer:
                return self.finish()
            if layer % 2 == 0:
                self.phase_lru()
                if self.stop == "lru":
                    return self.finish()
            with ExitStack() as lst:
                self.midw = self.midw_alloc(lst, layer)
                self.phase_attn(layer)
                if self.stop == "attn%d" % layer:
                    return self.finish()
                self.phase_mid(layer)
                self.midw = None
            if self.stop == "mid%d" % layer:
                return self.finish()
            if layer + 1 < DEPTH:
                self._nxt = ExitStack()
                self.WA = self.sb(self._nxt, "WA", [128, 32768], BF16)
            self.phase_mlp(layer)
            if self.stop == "mlp%d" % layer:
                return self.finish()
        return self.finish()

    def finish(self):
        self.p.flush()
        self.gst.close()
        self.p.close()
        return self.nc

    def consts(self):
        p = self.p
        ident, identf, tri, sel = self.ident, self.identf, self.tri, self.sel
        with ExitStack() as st:
            tmp = self.sb(st, "ctmp", [128, 128], F32)
            tb = Buf("ctmp")
            p.op("pool", lambda e: e.memset(tmp[:, :], 1.0), writes=[tb])
            p.op("pool", lambda e: e.affine_select(out=identf[:, :], in_=tmp[:, :], pattern=[[-1, 128]],
                                                    compare_op=ALU.is_equal, fill=0.0, base=0, channel_multiplier=1),
                 reads=[tb], writes=[self.identfb])
            p.op("pool", lambda e: e.tensor_copy(out=ident[:, :], in_=identf[:, :]), reads=[self.identfb],
                 writes=[self.identb])
            p.op("pool", lambda e: e.affine_select(out=tri[:, :], in_=tmp[:, :], pattern=[[1, 128]],
                                                    compare_op=ALU.is_ge, fill=0.0, base=0, channel_multiplier=-1),
                 reads=[tb], writes=[self.trib])
            p.op("pool", lambda e: e.affine_select(out=sel[:, :], in_=tmp[:, :], pattern=[[0, 128]],
                                                    compare_op=ALU.is_equal, fill=0.0, base=-127, channel_multiplier=1),
                 reads=[tb], writes=[self.selb])
            p.flush()

    def phase_mem(self):
        with ExitStack() as st:
            for _ in self.mem_gen(st):
                pass
            self.p.flush()

    def mem_gen(self, st, npso=4, nstg=3):
        p, I = self.p, self.I
        x = self.sb(st, "memx", [128, 2, D], F32)
        xb = Buf("memx")
        xn = self.sb(st, "memn", [128, 2, D], BF16)
        xnb = Buf("memn")
        xT = self.sb(st, "memT", [128, 8, NMEM], BF16)
        xTb = Buf("memT")
        junk = self.sb(st, "junk", [128, D], BF16)
        junkb = Buf("junk", strict=True)
        ss = self.sb(st, "ss", [128, 4], F32)
        ssb = Buf("ss")
        rstd = self.sb(st, "rstd", [128, 4], F32)
        rstdb = Buf("rstd")
        pst = [self.ps(st, "pst", [128, 8, 128], BF16) for _ in range(2)]
        pstb = [Buf("pst") for _ in range(2)]
        pso = [self.ps(st, "pso", [128, 512], F32) for _ in range(npso)]
        psob = [Buf("pso") for _ in range(npso)]
        gcol, gb = self.load_col(st, "gmem", I["g_mem"], 8)
        wkv = self.sb(st, "wkv", [128, 8, 2 * D], BF16)
        self._stg = [self.sb(st, "stg", [128, 2048], F32) for _ in range(nstg)]
        self._stgb = [Buf("stg%d" % i) for i in range(nstg)]
        self._stg_st = st
        p.dma("sp", x[:, :, :], I["mem"].rearrange("(j p) d -> p j d", p=128), writes=[xb])
        yield
        self.rms_cast(x, xb, xn, xnb, 2, junk, junkb, ss, ssb, rstd, rstdb)
        yield
        self.transpose_to(xn, xnb, xT, xTb, 2, 8, pst, pstb, self.ident, self.identb)
        yield
        k = 0
        wb = Buf("wkv")
        for l in range(DEPTH):
            yield from self.load_w_gen(st, wkv, wb, I["xattn_wkv"][l], 8, 2 * D, gcol=gcol, gbuf=gb)
            kT, kTb = self.kmemT[l], self.kmemTb[l]
            vE, vEb = self.vmemE[l], self.vmemEb[l]
            p.op("pool", lambda e, vE=vE: e.memset(vE[:, :, :, 256:257], 1.0), writes=[vEb])
            for ct in range(8):
                po, pob = pso[k % npso], psob[k % npso]
                k += 1
                for kc in range(8):
                    p.op("pe", lambda e, po=po, kc=kc, ct=ct: e.matmul(
                        out=po[:, 0:NMEM], lhsT=wkv[:, kc, ct * 128:(ct + 1) * 128], rhs=xT[:, kc, :],
                        start=(kc == 0), stop=(kc == 7)), reads=[wb, xTb], writes=[pob])
                p.op("act", lambda e, po=po, ct=ct, kT=kT: e.copy(out=kT[:, ct, :], in_=po[:, 0:NMEM]),
                     reads=[pob], writes=[kTb])
                yield
            for mc in range(2):
                for c2 in range(2):
                    po, pob = pso[k % npso], psob[k % npso]
                    k += 1
                    for kc in range(8):
                        p.op("pe", lambda e, po=po, kc=kc, mc=mc, c2=c2: e.matmul(
                            out=po[:, :], lhsT=xT[:, kc, mc * 128:(mc + 1) * 128],
                            rhs=wkv[:, kc, D + c2 * 512:D + (c2 + 1) * 512],
                            start=(kc == 0), stop=(kc == 7)), reads=[wb, xTb], writes=[pob])
                    p.op("dve", lambda e, po=po, mc=mc, c2=c2, vE=vE: e.tensor_copy(
                        out=vE[:, mc, 2 * c2:2 * c2 + 2, 0:256],
                        in_=po[:, :].rearrange("p (h d) -> p h d", h=2)),
                        reads=[pob], writes=[vEb])
                    yield

    def phase_inproj(self, layer):
        p, I, T, Tb = self.p, self.I, self.T, self.Tb
        even = (layer % 2 == 0)
        NIN = EVEN_IN if even else 3 * D
        hsrc = I["x"] if layer == 0 else T["h"]
        with ExitStack() as st:
            winb = Buf("win")
            if self.win_pre is not None:
                win = self.win_pre
            else:
                gcol, gb = self.load_col(st, "gmix", I["mix_norm_g"][layer], 8)
                win = self.sb(st, "win", [128, 8, NIN], BF16)
                self.load_w(st, win, winb, I["w_in_even"] if even else I["w_in_odd"], 8, NIN, gcol=gcol, gbuf=gb,
                            ncol_chunk=1536)
            xs = [self.sb(st, "x", [128, 4, D], F32) for _ in range(2)]
            xsb = [Buf("x") for _ in range(2)]
            xn = self.sb(st, "xn", [128, 4, D], BF16)
            xnb = Buf("xn")
            xTs = [self.sb(st, "xT", [128, 8, TT], BF16) for _ in range(2)]
            xTsb = [Buf("xT") for _ in range(2)]
            junk = self.sb(st, "junk", [128, D], BF16)
            junkb = Buf("junk", strict=True)
            ss = self.sb(st, "ss", [128, 4], F32)
            ssb = Buf("ss")
            rstd = self.sb(st, "rstd", [128, 4], F32)
            rstdb = Buf("rstd")
            pst = [self.ps(st, "pst", [128, 8, 128], BF16) for _ in range(2)]
            pstb = [Buf("pst") for _ in range(2)]
            pso = [self.ps(st, "pso", [128, 512], F32) for _ in range(6)]
            psob = [Buf("pso") for _ in range(6)]
            NO = 4
            of32 = [self.sb(st, "of32", [128, TT], F32) for _ in range(NO)]
            of32b = [Buf("of32") for _ in range(NO)]
            obf = [self.sb(st, "obf", [128, TT], BF16) for _ in range(NO)]
            obfb = [Buf("obf") for _ in range(NO)]
            dv = 64 if even else 128
            vts = [self.sb(st, "vt", [128, 8, dv + 1], BF16) for _ in range(2)]
            vtsb = [Buf("vt") for _ in range(2)]
            for v_ in vts:
                p.op("pool", lambda e, v_=v_: e.memset(v_[:, :, dv:dv + 1], 1.0), writes=[vtsb[vts.index(v_)]])
            vE = T["vE0"] if even else T["vE1"]
            vEb = Tb["vE0"] if even else Tb["vE1"]
            k = 0
            io = 0
            iv = 0
            ev = 0
            def load(tt):
                p.dma("sp", xs[tt % 2][:, :, :], hsrc[tt * TT:(tt + 1) * TT, :].rearrange("(j p) d -> p j d", p=128),
                      writes=[xsb[tt % 2]])

            load(0)
            for tt in range(NTT):
                x, xb = xs[tt % 2], xsb[tt % 2]
                xT, xTb = xTs[tt % 2], xTsb[tt % 2]
                if tt + 1 < NTT:
                    load(tt + 1)
                self.rms_cast(x, xb, xn, xnb, 4, junk, junkb, ss, ssb, rstd, rstdb)
                self.transpose_to(xn, xnb, xT, xTb, 4, 8, pst, pstb, self.ident, self.identb)
                if even:
                    tiles = [(c * 128, 128, 0, c * 128) for c in range(8)]
                    tiles += [(1024 + c * 128, 128, 1, c * 128) for c in range(4)]
                    tiles += [(1536 + c * 128, 128, 2, c * 128) for c in range(4)]
                    tiles += [(2560, 8, 3, 0)]
                    vcol = 2048
                    nvt = 1
                else:
                    tiles = [(c * 128, 128, 1, c * 128) for c in range(8)]
                    tiles += [(1024 + c * 128, 128, 2, c * 128) for c in range(8)]
                    vcol = 2048
                    nvt = 2
                for (c0, ncol, kind, r0) in tiles:
                    po, pob = pso[k % 6], psob[k % 6]
                    k += 1
                    for kc in range(8):
                        p.op("pe", lambda e, po=po, kc=kc, c0=c0, ncol=ncol, xT=xT: e.matmul(
                            out=po[0:ncol, :], lhsT=win[:, kc, c0:c0 + ncol], rhs=xT[:, kc, :],
                            start=(kc == 0), stop=(kc == 7)), reads=[winb, xTb], writes=[pob])
                    eng = ("act", "dve")[ev % 2]
                    ev += 1
                    if kind in (0, 3):
                        o, ob = of32[io % NO], of32b[io % NO]
                    else:
                        o, ob = obf[io % NO], obfb[io % NO]
                    io += 1
                    scale = 0.125 if kind == 1 else 1.0
                    if eng == "act":
                        p.op("act", lambda e, o=o, po=po, ncol=ncol, scale=scale: e.activation(
                            out=o[0:ncol, :], in_=po[0:ncol, :], func=AF.Copy, scale=scale),
                            reads=[pob], writes=[ob])
                    else:
                        p.op("dve", lambda e, o=o, po=po, ncol=ncol, scale=scale: e.tensor_scalar(
                            out=o[0:ncol, :], in0=po[0:ncol, :], scalar1=scale, scalar2=None, op0=ALU.mult),
                            reads=[pob], writes=[ob])
                    if kind == 0:
                        dst, dstb = T["xgT"], Tb["xgT"]
                    elif kind == 1:
                        dst, dstb = T["qT"], Tb["qT"]
                    elif kind == 2:
                        dst, dstb = T["kT"], Tb["kT"]
                    else:
                        dst, dstb = T["fT"], Tb["fT"]
                    p.dma("act" if False else "sp", dst[r0:r0 + ncol, tt * TT:(tt + 1) * TT], o[0:ncol, :], reads=[ob])
                for j in range(4):
                    vt, vtb = vts[iv % 2], vtsb[iv % 2]
                    iv += 1
                    for vc in range(nvt):
                        po, pob = pso[k % 6], psob[k % 6]
                        k += 1
                        for kc in range(8):
                            p.op("pe", lambda e, po=po, kc=kc, j=j, vc=vc, xT=xT: e.matmul(
                                out=po[:, :], lhsT=xT[:, kc, j * 128:(j + 1) * 128],
                                rhs=win[:, kc, vcol + vc * 512:vcol + (vc + 1) * 512],
                                start=(kc == 0), stop=(kc == 7)), reads=[winb, xTb], writes=[pob])
                        nh = 512 // dv
                        eng = ("act", "dve")[ev % 2]
                        ev += 1
                        if eng == "act":
                            p.op("act", lambda e, po=po, vt=vt, vc=vc, nh=nh: e.copy(
                                out=vt[:, vc * nh:(vc + 1) * nh, 0:dv], in_=po[:, :].rearrange("p (h d) -> p h d", h=nh)),
                                reads=[pob], writes=[vtb])
                        else:
                            p.op("dve", lambda e, po=po, vt=vt, vc=vc, nh=nh: e.tensor_copy(
                                out=vt[:, vc * nh:(vc + 1) * nh, 0:dv], in_=po[:, :].rearrange("p (h d) -> p h d", h=nh)),
                                reads=[pob], writes=[vtb])
                    r = tt * TT + j * 128
                    p.dma("sp", vE[r:r + 128, :, :], vt[:, :, :], reads=[vtb])
            p.flush()

    def phase_lru(self):
        p, I, T, Tb = self.p, self.I, self.T, self.Tb
        with ExitStack() as st:
            cw = self.sb(st, "cw", [128, 4, 4], F32)
            cwb = Buf("cw")
            for c in range(4):
                for k in range(4):
                    p.dma("sp", cw[:, c, k:k + 1],
                          I["conv_w"][k, c * 128:(c + 1) * 128].rearrange("(p o) -> p o", o=1), writes=[Buf()])
            cb, cbb = self.load_col(st, "cb", I["conv_b"], 4)
            br, brb = self.load_col(st, "br", I["b_rgate"], 4)
            bi, bib = self.load_col(st, "bi", I["b_igate"], 4)
            lam, lamb = self.load_col(st, "lam", I["lru_lambda"], 4)
            ls8 = self.sb(st, "ls8", [128, 4], F32)
            ls16 = self.sb(st, "ls16", [128, 4], F32)
            lsb = Buf("ls")
            p.op("act", lambda e: e.activation(out=ls8[:, :], in_=lam[:, :], func=AF.Exp, scale=-1.0),
                 reads=[lamb], writes=[lsb])
            p.op("act", lambda e: e.activation(out=ls8[:, :], in_=ls8[:, :], func=AF.Ln, bias=1.0),
                 reads=[lsb], writes=[lsb])
            p.op("dve", lambda e: e.tensor_scalar(out=ls16[:, :], in0=ls8[:, :], scalar1=-16.0, scalar2=None,
                                                  op0=ALU.mult), reads=[lsb], writes=[lsb])
            p.op("dve", lambda e: e.tensor_scalar(out=ls8[:, :], in0=ls8[:, :], scalar1=-8.0, scalar2=None,
                                                  op0=ALU.mult), reads=[lsb], writes=[lsb])
            wst = self.sb(st, "wst", [128, 2, 4, 128], F32)
            wstb = Buf("wst")
            wg = self.sb(st, "wg", [128, 2, 4, 128], BF16)
            wgb = Buf("wg")
            p.op("pool", lambda e: e.memset(wst[:, :, :, :], 0.0), writes=[wstb])
            p.flush()
            for gi, nm in enumerate(("w_rgate", "w_igate")):
                for c in range(4):
                    for s2 in range(2):
                        p.dma("sp", wst[s2 * 64:(s2 + 1) * 64, gi, c, s2 * 64:(s2 + 1) * 64], I[nm][2 * c + s2],
                              writes=[Buf()])
            p.flush()
            p.op("pool", lambda e: e.tensor_copy(out=wg[:, :, :, :], in_=wst[:, :, :, :]), writes=[wgb])
            bufs = [self.sb(st, "L%d" % i, [128, S], F32) for i in range(6)]
            bb = [Buf("L%d" % i) for i in range(6)]
            X, G, XC, R, Ig, A = bufs
            Xb, Gb, XCb, Rb, Ib, Ab = bb
            xcb16 = self.sb(st, "xcb16", [128, S], BF16)
            xcb16b = Buf("xcb16")
            yb16 = self.sb(st, "yb16", [128, S], BF16)
            yb16b = Buf("yb16")
            NPG = 2 if self.fuse_mem else 4
            pss = [self.ps(st, "psg", [128, 512], F32) for _ in range(NPG)]
            pssb = [Buf("psg") for _ in range(NPG)]
            if self.fuse_mem:
                p = _Ticker(self.p, self.mem_gen(st, npso=3, nstg=2), 2)
            k = 0
            for c in range(4):
                p.dma("sp", X[:, :], T["xgT"][c * 128:(c + 1) * 128, :], writes=[Xb])
                p.dma("sp", G[:, :], T["xgT"][512 + c * 128:512 + (c + 1) * 128, :], writes=[Gb])
                p.op("dve", lambda e, c=c: e.tensor_scalar(out=XC[:, :], in0=X[:, :], scalar1=cw[:, c, 3:4],
                                                           scalar2=cb[:, c:c + 1], op0=ALU.mult, op1=ALU.add),
                     reads=[Xb, cbb], writes=[XCb])
                for kk in range(3):
                    s = 3 - kk
                    p.op("dve", lambda e, c=c, kk=kk, s=s: e.scalar_tensor_tensor(
                        out=XC[:, s:S], in0=X[:, 0:S - s], scalar=cw[:, c, kk:kk + 1], in1=XC[:, s:S],
                        op0=ALU.mult, op1=ALU.add), reads=[Xb, XCb], writes=[XCb])
                p.op("act", lambda e: e.copy(out=xcb16[:, :], in_=XC[:, :]), reads=[XCb], writes=[xcb16b])
                for tt in range(NTT):
                    for gi in range(2):
                        ps_, psb_ = pss[k % NPG], pssb[k % NPG]
                        k += 1
                        dst, dstb = (R, Rb) if gi == 0 else (Ig, Ib)
                        bias = br if gi == 0 else bi
                        p.op("pe", lambda e, ps_=ps_, gi=gi, c=c, tt=tt: e.matmul(
                            out=ps_[:, :], lhsT=wg[:, gi, c, :], rhs=xcb16[:, tt * TT:(tt + 1) * TT],
                            start=True, stop=True), reads=[wgb, xcb16b], writes=[psb_])
                        p.op("act", lambda e, ps_=ps_, dst=dst, bias=bias, c=c, tt=tt: e.activation(
                            out=dst[:, tt * TT:(tt + 1) * TT], in_=ps_[:, :], func=AF.Sigmoid,
                            bias=bias[:, c:c + 1]), reads=[psb_, brb, bib], writes=[dstb])
                p.op("act", lambda e, c=c: e.activation(out=A[:, :], in_=R[:, :], func=AF.Exp, scale=ls8[:, c:c + 1]),
                     reads=[Rb, lsb], writes=[Ab])
                p.op("act", lambda e, c=c: e.activation(out=X[:, :], in_=R[:, :], func=AF.Exp, scale=ls16[:, c:c + 1]),
                     reads=[Rb, lsb], writes=[Xb])
                p.op("act", lambda e: e.activation(out=X[:, :], in_=X[:, :], func=AF.Sqrt, scale=-1.0, bias=1.0),
                     reads=[Xb], writes=[Xb])
                p.op("pool", lambda e: e.tensor_tensor(out=Ig[:, :], in0=Ig[:, :], in1=XC[:, :], op=ALU.mult),
                     reads=[Ib, XCb], writes=[Ib])
                p.op("dve", lambda e: e.tensor_tensor(out=Ig[:, :], in0=Ig[:, :], in1=X[:, :], op=ALU.mult),
                     reads=[Ib, Xb], writes=[Ib])
                p.op("dve", lambda e: e.tensor_tensor_scan(out=XC[:, :], data0=A[:, :], data1=Ig[:, :], initial=0.0,
                                                           op0=ALU.mult, op1=ALU.add),
                     reads=[Ab, Ib], writes=[XCb])
                p.op("pool", lambda e: e.tensor_tensor(out=R[:, :], in0=G[:, :], in1=G[:, :], op=ALU.mult),
                     reads=[Gb], writes=[Rb])
                p.op("pool", lambda e: e.tensor_scalar(out=R[:, :], in0=R[:, :], scalar1=0.044715, scalar2=1.0,
                                                       op0=ALU.mult, op1=ALU.add), reads=[Rb], writes=[Rb])
                p.op("pool", lambda e: e.tensor_tensor(out=R[:, :], in0=R[:, :], in1=G[:, :], op=ALU.mult),
                     reads=[Rb, Gb], writes=[Rb])
                p.op("act", lambda e: e.activation(out=R[:, :], in_=R[:, :], func=AF.Sigmoid,
                                                   scale=2.0 * math.sqrt(2.0 / math.pi)), reads=[Rb], writes=[Rb])
                p.op("dve", lambda e: e.tensor_tensor(out=XC[:, :], in0=XC[:, :], in1=G[:, :], op=ALU.mult),
                     reads=[XCb, Gb], writes=[XCb])
                p.op("dve", lambda e: e.tensor_tensor(out=yb16[:, :], in0=XC[:, :], in1=R[:, :], op=ALU.mult),
                     reads=[XCb, Rb], writes=[yb16b])
                p.dma("sp", T["ylruT"][c * 128:(c + 1) * 128, :], yb16[:, :], reads=[yb16b])
            if self.fuse_mem:
                p.drain()
            p.flush()

    def phase_attn(self, layer):
        p, I, T, Tb = self.p, self.I, self.T, self.Tb
        even = (layer % 2 == 0)
        dv = 64 if even else 128
        nmaps = 1 if even else 2
        qrows = 64 if even else 128
        vE = T["vE0"] if even else T["vE1"]
        lam_init = 0.8 - 0.6 * math.exp(-0.3 * layer)
        with ExitStack() as st:
            pss = [self.ps(st, "pss", [128, 512], F32) for _ in range(4)]
            pssb = [Buf("pss") for _ in range(4)]
            pos = [self.ps(st, "pos", [128, 512], F32) for _ in range(4)]
            posb = [Buf("pos") for _ in range(4)]
            if even:
                biasAll = self.sb(st, "biasAll", [128, 8, 8, 32], F32)
                biasb = Buf("biasAll")
                with ExitStack() as s2:
                    f = self.sb(s2, "f", [8, S], F32)
                    fb = Buf("f")
                    ones = self.sb(s2, "ones", [8, S], F32)
                    onesb = Buf("ones")
                    cp = self.sb(s2, "cp", [8, S], F32)
                    cpb = Buf("cp")
                    fbias = self.sb(s2, "fbias", [8, 1], F32)
                    fbiasb = Buf("fbias")
                    cpT = self.sb(s2, "cpT", [128, 32, 8], F32)
                    cpTb = Buf("cpT")
                    cpL = self.sb(s2, "cpL", [128, 32, 8], F32)
                    cpLb = Buf("cpL")
                    p.dma("sp", f[:, :], T["fT"][:, :], writes=[fb])
                    p.dma("sp", fbias[:, :], I["fox_forget_b"].rearrange("(p o) -> p o", o=1), writes=[fbiasb])
                    p.op("pool", lambda e: e.memset(ones[:, :], 1.0), writes=[onesb])
                    p.op("dve", lambda e: e.tensor_scalar(out=fbias[:, :], in0=fbias[:, :], scalar1=-1.0, scalar2=None,
                                                          op0=ALU.mult), reads=[fbiasb], writes=[fbiasb])
                    p.op("act", lambda e: e.activation(out=f[:, :], in_=f[:, :], func=AF.Exp, scale=-1.0,
                                                       bias=fbias[:, 0:1]), reads=[fb, fbiasb], writes=[fb])
                    p.op("act", lambda e: e.activation(out=f[:, :], in_=f[:, :], func=AF.Ln, bias=1.0),
                         reads=[fb], writes=[fb])
                    p.op("dve", lambda e: e.tensor_tensor_scan(out=cp[:, :], data0=ones[:, :], data1=f[:, :],
                                                               initial=0.0, op0=ALU.mult, op1=ALU.add),
                         reads=[fb, onesb], writes=[cpb])
                    pc = pss[0]
                    pcv = pc[:, 0:256].rearrange("p (j h) -> p j h", h=8)
                    for j in range(NBLK):
                        p.op("pe", lambda e, j=j: e.transpose(out=pcv[:, j, :], in_=cp[0:8, j * 128:(j + 1) * 128],
                                                              identity=self.identf[0:8, 0:8]),
                             reads=[cpb, self.identfb], writes=[pssb[0]])
                    p.op("act", lambda e: e.copy(out=cpT[:, :, :], in_=pcv), reads=[pssb[0]], writes=[cpTb])
                    p.op("pe", lambda e: e.matmul(out=pss[1][:, 0:256], lhsT=self.sel[:, :],
                                                  rhs=cpT[:, :, :].rearrange("p j h -> p (j h)"), start=True, stop=True),
                         reads=[cpTb, self.selb], writes=[pssb[1]])
                    p.op("act", lambda e: e.copy(out=cpL[:, :, :].rearrange("p j h -> p (j h)"), in_=pss[1][:, 0:256]),
                         reads=[pssb[1]], writes=[cpLb])
                    for h in range(8):
                        for qt in range(NTT):
                            p.op("dve", lambda e, h=h, qt=qt: e.tensor_scalar(
                                out=biasAll[:, h, qt, :], in0=cpT[:, :, h], scalar1=cpL[:, 4 * qt + 3, h:h + 1],
                                scalar2=None, op0=ALU.subtract), reads=[cpTb, cpLb], writes=[biasb])
                    p.flush()
            else:
                nlam = self.sb(st, "nlam", [128, 1], F32)
                nlamb = Buf("nlam")
                with ExitStack() as s2:
                    lt = self.sb(s2, "lt", [128, 4, 64], F32)
                    ltb = Buf("lt")
                    ssum = self.sb(s2, "ssum", [128, 2], F32)
                    ssumb = Buf("ssum")
                    for i, nm in enumerate(("lambda_q1", "lambda_k1", "lambda_q2", "lambda_k2")):
                        p.dma("sp", lt[:, i, :], I[nm].partition_broadcast(128), writes=[Buf()])
                    p.flush()
                    p.op("dve", lambda e: e.tensor_tensor(out=lt[:, 0, :], in0=lt[:, 0, :], in1=lt[:, 1, :], op=ALU.mult),
                         writes=[ltb])
                    p.op("dve", lambda e: e.tensor_tensor(out=lt[:, 2, :], in0=lt[:, 2, :], in1=lt[:, 3, :], op=ALU.mult),
                         writes=[ltb])
                    p.op("dve", lambda e: e.reduce_sum(out=ssum[:, 0:1], in_=lt[:, 0, :], axis=AX.X), reads=[ltb],
                         writes=[ssumb])
                    p.op("dve", lambda e: e.reduce_sum(out=ssum[:, 1:2], in_=lt[:, 2, :], axis=AX.X), reads=[ltb],
                         writes=[ssumb])
                    p.op("act", lambda e: e.activation(out=ssum[:, :], in_=ssum[:, :], func=AF.Exp), reads=[ssumb],
                         writes=[ssumb])
                    p.op("dve", lambda e: e.tensor_tensor(out=nlam[:, :], in0=ssum[:, 1:2], in1=ssum[:, 0:1],
                                                          op=ALU.subtract), reads=[ssumb], writes=[nlamb])
                    p.op("dve", lambda e: e.tensor_scalar(out=nlam[:, :], in0=nlam[:, :], scalar1=-lam_init, scalar2=None,
                                                          op0=ALU.add), reads=[nlamb], writes=[nlamb])
                    p.flush()
            qh = [self.sb(st, "qh", [128, S], BF16) for _ in range(2)]
            kh = [self.sb(st, "kh", [128, S], BF16) for _ in range(2)]
            vh = [self.sb(st, "vh", [128, NBLK, dv + 1], BF16) for _ in range(2)]
            qhb = [Buf("qh") for _ in range(2)]
            khb = [Buf("kh") for _ in range(2)]
            vhb = [Buf("vh") for _ in range(2)]
            PT = [self.sb(st, "PT", [128, NBLK, 512], BF16) for _ in range(2)]
            PTb = [[Buf("PT") for _ in range(NBLK)] for _ in range(2)]
            Yt = [self.sb(st, "Yt", [128, 4, dv], BF16) for _ in range(2)]
            Ytb = [Buf("Yt") for _ in range(2)]
            sm = [self.sb(st, "sm", [128, 8], F32) for _ in range(4)]
            smb = [Buf("sm") for _ in range(4)]
            if not even:
                O0 = [self.sb(st, "O0", [128, 4, 128], F32) for _ in range(2)]
                O0b = [Buf("O0") for _ in range(2)]
                Of = [self.sb(st, "Of", [128, 128], F32) for _ in range(2)]
                Ofb = [Buf("Of") for _ in range(2)]
                junk = self.sb(st, "junk", [128, 128], F32)
                junkb = Buf("junk", strict=True)

            if even:
                khm = [[kh[0]], [kh[1]]]
                for i_ in range(2):
                    p.op("pool", lambda e, i_=i_: e.memset(kh[i_][64:128, :], 0.0), writes=[khb[i_]])
                    p.op("pool", lambda e, i_=i_: e.memset(qh[i_][64:128, :], 0.0), writes=[qhb[i_]])
            else:
                kh2 = [self.sb(st, "kh2", [128, S], BF16) for _ in range(2)]
                khm = [[kh[0], kh2[0]], [kh[1], kh2[1]]]
                for i_ in range(2):
                    p.op("pool", lambda e, i_=i_: e.memset(kh[i_][64:128, :], 0.0), writes=[khb[i_]])
                    p.op("pool", lambda e, i_=i_: e.memset(kh2[i_][0:64, :], 0.0), writes=[khb[i_]])
            p.flush()

            def load(h):
                i = h % 2
                p.dma("sp", qh[i][0:qrows, :], T["qT"][h * qrows:(h + 1) * qrows, :], writes=[qhb[i]])
                p.dma("sp", kh[i][0:64, :], T["kT"][h * qrows:h * qrows + 64, :], writes=[khb[i]])
                if not even:
                    p.dma("sp", kh2[i][64:128, :], T["kT"][h * qrows + 64:h * qrows + 128, :], writes=[khb[i]])
                p.dma("sp", vh[i][:, :, :], vE[:, h, :].rearrange("(j p) c -> p j c", p=128), writes=[vhb[i]])

            cnt = {"kq": 0, "ko": 0, "ky": 0, "ksm": 0, "kO": 0}
            cur_y = [None, None]

            def gen_qk(h, qt, m, pt, ptb):
                i = h % 2
                for j in range(4 * qt + 4):
                    r = j - 4 * qt
                    q0 = max(r, 0) * 128
                    ps_, psb_ = pss[cnt["kq"] % 4], pssb[cnt["kq"] % 4]
                    cnt["kq"] += 1
                    p.op("pe", lambda e, ps_=ps_, i=i, m=m, j=j, q0=q0, qt=qt: e.matmul(
                        out=ps_[:, q0:512], lhsT=khm[i][m][:, j * 128:(j + 1) * 128],
                        rhs=qh[i][:, qt * TT + q0:(qt + 1) * TT], start=True, stop=True),
                        reads=[khb[i], qhb[i]], writes=[psb_])
                    if even:
                        p.op("act", lambda e, ps_=ps_, pt=pt, j=j, q0=q0, h=h, qt=qt: e.activation(
                            out=pt[:, j, q0:512], in_=ps_[:, q0:512], func=AF.Exp,
                            bias=biasAll[:, h, qt, j:j + 1]), reads=[psb_, biasb], writes=[ptb[j]])
                    else:
                        p.op("act", lambda e, ps_=ps_, pt=pt, j=j, q0=q0: e.activation(
                            out=pt[:, j, q0:512], in_=ps_[:, q0:512], func=AF.Exp),
                            reads=[psb_], writes=[ptb[j]])
                    if r >= 0:
                        p.op("pool", lambda e, pt=pt, j=j, q0=q0: e.tensor_tensor(
                            out=pt[:, j, q0:q0 + 128], in0=pt[:, j, q0:q0 + 128], in1=self.tri[:, :], op=ALU.mult),
                            reads=[ptb[j], self.trib], writes=[ptb[j]])
                    yield

            def gen_pv(h, qt, m, pt, ptb):
                i = h % 2
                if m == nmaps - 1:
                    cur_y[0], cur_y[1] = Yt[cnt["ky"] % 2], Ytb[cnt["ky"] % 2]
                    cnt["ky"] += 1
                yt, ytb = cur_y
                for qb in range(4):
                    po, pob = pos[cnt["ko"] % 4], posb[cnt["ko"] % 4]
                    cnt["ko"] += 1
                    nk = 4 * qt + qb + 1
                    for j in range(nk):
                        p.op("pe", lambda e, po=po, pt=pt, j=j, qb=qb, i=i, nk=nk: e.matmul(
                            out=po[:, 0:dv + 1], lhsT=pt[:, j, qb * 128:(qb + 1) * 128], rhs=vh[i][:, j, :],
                            start=(j == 0), stop=(j == nk - 1)), reads=[ptb[j], vhb[i]], writes=[pob])
                        if j % 4 == 3:
                            yield
                    s_, sb_ = sm[cnt["ksm"] % 4], smb[cnt["ksm"] % 4]
                    cnt["ksm"] += 1
                    p.op("dve", lambda e, po=po, s_=s_: e.reciprocal(out=s_[:, 0:1], in_=po[:, dv:dv + 1]),
                         reads=[pob], writes=[sb_])
                    if even:
                        p.op("dve", lambda e, po=po, s_=s_, yt=yt, qb=qb: e.tensor_scalar(
                            out=yt[:, qb, :], in0=po[:, 0:dv], scalar1=s_[:, 0:1], scalar2=None, op0=ALU.mult),
                            reads=[pob, sb_], writes=[ytb])
                    elif m == 0:
                        kO = cnt["kO"]
                        o0, o0b = O0[(kO // 4) % 2], O0b[(kO // 4) % 2]
                        p.op("dve", lambda e, po=po, s_=s_, o0=o0, qb=qb: e.tensor_scalar(
                            out=o0[:, qb, :], in0=po[:, 0:dv], scalar1=s_[:, 0:1], scalar2=None, op0=ALU.mult),
                            reads=[pob, sb_], writes=[o0b])
                    else:
                        kO = cnt["kO"]
                        o0, o0b = O0[(kO // 4) % 2], O0b[(kO // 4) % 2]
                        of_, ofb_ = Of[kO % 2], Ofb[kO % 2]
                        cnt["kO"] += 1
                        p.op("dve", lambda e, s_=s_: e.tensor_tensor(out=s_[:, 1:2], in0=s_[:, 0:1], in1=nlam[:, 0:1],
                                                                    op=ALU.mult), reads=[sb_, nlamb], writes=[sb_])
                        p.op("dve", lambda e, po=po, s_=s_, o0=o0, of_=of_, qb=qb: e.scalar_tensor_tensor(
                            out=of_[:, :], in0=po[:, 0:dv], scalar=s_[:, 1:2], in1=o0[:, qb, :],
                            op0=ALU.mult, op1=ALU.add), reads=[pob, sb_, o0b], writes=[ofb_])
                        p.op("dve", lambda e, of_=of_, s_=s_: e.scalar_tensor_tensor(
                            out=junk[:, :], in0=of_[:, :], scalar=1.0, in1=of_[:, :], op0=ALU.mult, op1=ALU.mult,
                            accum_out=s_[:, 2:3]), reads=[ofb_], writes=[junkb, sb_])
                        p.op("dve", lambda e, s_=s_: e.tensor_scalar(
                            out=s_[:, 3:4], in0=s_[:, 2:3], scalar1=1.0 / 128, scalar2=EPS, op0=ALU.mult,
                            op1=ALU.add), reads=[sb_], writes=[sb_])
                        p.op("pool", lambda e, s_=s_: e.tensor_tensor(out=s_[:, 5:6], in0=s_[:, 3:4], in1=mhalf[:, 0:1],
                                                                     op=ALU.pow), reads=[sb_, mhalfb], writes=[sb_])
                        p.op("dve", lambda e, s_=s_, of_=of_, yt=yt, qb=qb: e.tensor_scalar(
                            out=yt[:, qb, :], in0=of_[:, :], scalar1=s_[:, 5:6], scalar2=None, op0=ALU.mult),
                            reads=[ofb_, sb_], writes=[ytb])
                    yield
                if m == nmaps - 1:
                    c0 = (512 + h * 64) if even else h * 128
                    p.dma("sp", T["y"][qt * TT:(qt + 1) * TT, c0:c0 + dv].rearrange("(j p) d -> p j d", p=128),
                          yt[:, :, :], reads=[ytb])
                yield

            if not even:
                mhalf = self.sb(st, "mhalf", [128, 1], F32)
                mhalfb = Buf("mhalf")
                p.op("pool", lambda e: e.memset(mhalf[:, :], -0.5), writes=[mhalfb])
            steps = [(h, qt, m) for h in range(8) for qt in range(NTT) for m in range(nmaps)]
            load(0)
            load(1)

            def drain(g):
                for _ in g:
                    pass

            g0 = gen_qk(*steps[0], PT[0], PTb[0])
            drain(g0)
            wg_ = self.midw["gen"] if self.midw else None
            for s, (h, qt, m) in enumerate(steps):
                if qt == 0 and m == 0 and 1 <= h and h + 1 < 8:
                    load(h + 1)
                if wg_ is not None and s >= 2:
                    if next(wg_, "done") == "done":
                        wg_ = None
                gp = gen_pv(h, qt, m, PT[s % 2], PTb[s % 2])
                gq = None
                if s + 1 < len(steps):
                    gq = gen_qk(*steps[s + 1], PT[(s + 1) % 2], PTb[(s + 1) % 2])
                alive_p, alive_q = True, gq is not None
                for _lead in range(2):
                    if alive_q and next(gq, "done") == "done":
                        alive_q = False
                while alive_p or alive_q:
                    if alive_q:
                        try:
                            next(gq)
                        except StopIteration:
                            alive_q = False
                    if alive_p:
                        try:
                            next(gp)
                        except StopIteration:
                            alive_p = False
            if wg_ is not None:
                for _ in wg_:
                    pass
            p.flush()

    def midw_alloc(self, st, layer):
        p, I = self.p, self.I
        even = (layer % 2 == 0)
        lam_init = 0.8 - 0.6 * math.exp(-0.3 * layer)
        M = {}
        names = ("wout", "wq", "wo") if even else ("wout", "wq")
        for n in names:
            M[n] = self.sb(st, n, [128, 8, D], BF16)
            M[n + "b"] = Buf(n)
        gcol, gb = self.load_col(st, "gx", I["xattn_norm_g"][layer], 8)
        M["gcol"], M["gb"] = gcol, gb
        items = []
        if even:
            for kc in range(8):
                items.append((M["wout"][:, kc, :], M["woutb"], I["w_out_even"][kc * 128:(kc + 1) * 128, :], None, None, 1.0))
        else:
            gd = self.sb(st, "gd", [128, 8], F32)
            gdb = Buf("gd")
            for kc in range(8):
                p.dma("sp", gd[:, kc:kc + 1], I["diff_norm_g"].rearrange("(p o) -> p o", o=1), writes=[gdb])
            for kc in range(8):
                items.append((M["wout"][:, kc, :], M["woutb"], I["w_out_odd"][kc * 128:(kc + 1) * 128, :],
                              gd[:, kc:kc + 1], gdb, 1.0 - lam_init))
        for kc in range(8):
            items.append((M["wq"][:, kc, :], M["wqb"], I["xattn_wq"][layer][kc * 128:(kc + 1) * 128, :],
                          gcol[:, kc:kc + 1], gb, 1.0))
        if even:
            for kc in range(8):
                items.append((M["wo"][:, kc, :], M["wob"], I["xattn_wo"][layer][kc * 128:(kc + 1) * 128, :], None, None, 1.0))
        NS = 3
        stgs = [self.sb(st, "pstg", [128, D], F32) for _ in range(NS)]
        stgb = [Buf("pstg") for _ in range(NS)]

        def gen():
            pend = []

            def cast(it, stg, sbf):
                dst, dstb, _, g, gbuf, mul2 = it
                if g is not None:
                    p.op("pool", lambda e: e.tensor_scalar(out=dst, in0=stg[:, :], scalar1=g, scalar2=float(mul2),
                                                           op0=ALU.mult, op1=ALU.mult), reads=[sbf, gbuf], writes=[dstb])
                else:
                    p.op("pool", lambda e: e.tensor_copy(out=dst, in_=stg[:, :]), reads=[sbf], writes=[dstb])

            for idx, it in enumerate(items):
                stg, sbf = stgs[idx % NS], stgb[idx % NS]
                p.dma("sp", stg[:, :], it[2], writes=[sbf])
                pend.append((it, stg, sbf))
                if len(pend) > NS - 1:
                    cast(*pend.pop(0))
                yield
            while pend:
                cast(*pend.pop(0))
                yield

        M["gen"] = gen()
        return M

    def phase_mid(self, layer):
        p, I, T, Tb = self.p, self.I, self.T, self.Tb
        even = (layer % 2 == 0)
        lam_init = 0.8 - 0.6 * math.exp(-0.3 * layer)
        hsrc = I["x"] if layer == 0 else T["h"]
        with ExitStack() as st:
            M = self.midw
            wout, woutb, wq, wqb = M["wout"], M["woutb"], M["wq"], M["wqb"]
            if "wo" in M:
                wo, wob = M["wo"], M["wob"]
            else:
                wo = self.sb(st, "wo", [128, 8, D], BF16)
                wob = Buf("wo")
                self.load_w(st, wo, wob, I["xattn_wo"][layer], 8, D, ncol_chunk=1024)
            kT, kTb = self.kmemT[layer], self.kmemTb[layer]
            vE, vEb = self.vmemE[layer], self.vmemEb[layer]
            xs = [self.sb(st, "x", [128, 4, D], F32) for _ in range(2)]
            xsb = [Buf("x") for _ in range(2)]
            yin = [self.sb(st, "yin", [128, 4, D], BF16) for _ in range(2)]
            yinb = [Buf("yin") for _ in range(2)]
            YT = [self.sb(st, "YT", [128, 8, TT], BF16) for _ in range(2)]
            YTb = [Buf("YT") for _ in range(2)]
            YTlb = [Buf("YTl") for _ in range(2)]
            xn = self.sb(st, "xn", [128, 4, D], BF16)
            xnb = Buf("xn")
            xT = self.sb(st, "xT", [128, 8, TT], BF16)
            xTb = Buf("xT")
            qT = self.sb(st, "qT", [128, 8, TT], BF16)
            qTb = Buf("qT")
            PTm = self.sb(st, "PTm", [128, 4, 2, TT], BF16)
            PTmb = [[Buf("PTm") for _ in range(2)] for _ in range(4)]
            junk = self.sb(st, "junk", [128, D], BF16)
            junkb = Buf("junk", strict=True)
            ss = self.sb(st, "ss", [128, 4], F32)
            ssb = Buf("ss")
            rstd = self.sb(st, "rstd", [128, 4], F32)
            rstdb = Buf("rstd")
            sm = [self.sb(st, "sm", [128, 2], F32) for _ in range(4)]
            smb = [Buf("sm") for _ in range(4)]
            pst = [self.ps(st, "pst", [128, 8, 128], BF16) for _ in range(2)]
            pstb = [Buf("pst") for _ in range(2)]
            pso = [self.ps(st, "pso", [128, 512], F32) for _ in range(4)]
            psob = [Buf("pso") for _ in range(4)]
            pos = [self.ps(st, "pos", [128, 512], F32) for _ in range(2)]
            posb = [Buf("pos") for _ in range(2)]

            def load(tt):
                i = tt % 2
                p.dma("sp", xs[i][:, :, :], hsrc[tt * TT:(tt + 1) * TT, :].rearrange("(j p) d -> p j d", p=128),
                      writes=[xsb[i]])
                if even:
                    p.dma("sp", YT[i][:, 0:4, :], T["ylruT"][:, tt * TT:(tt + 1) * TT].rearrange("(c p) t -> p c t", p=128),
                          writes=[YTlb[i]])
                    p.dma("sp", yin[i][:, :, 0:512],
                          T["y"][tt * TT:(tt + 1) * TT, 512:1024].rearrange("(j p) d -> p j d", p=128), writes=[yinb[i]])
                else:
                    p.dma("sp", yin[i][:, :, :], T["y"][tt * TT:(tt + 1) * TT, :].rearrange("(j p) d -> p j d", p=128),
                          writes=[yinb[i]])

            k = 0
            ko = 0
            ksm = 0
            ev = 0
            load(0)
            for tt in range(NTT):
                if tt + 1 < NTT:
                    load(tt + 1)
                i = tt % 2
                x, xb = xs[i], xsb[i]
                if even:
                    self.transpose_to(yin[i], yinb[i], YT[i][:, 4:8, :], YTb[i], 4, 4, pst, pstb, self.ident, self.identb)
                    ytreads = [YTb[i], YTlb[i]]
                else:
                    self.transpose_to(yin[i], yinb[i], YT[i], YTb[i], 4, 8, pst, pstb, self.ident, self.identb)
                    ytreads = [YTb[i]]

                def proj_add(srcT, srcreads, w, wb_):
                    nonlocal k
                    for j in range(4):
                        for ct in range(2):
                            po, pob = pso[k % 4], psob[k % 4]
                            k += 1
                            for kc in range(8):
                                p.op("pe", lambda e, po=po, kc=kc, j=j, ct=ct: e.matmul(
                                    out=po[:, :], lhsT=srcT[:, kc, j * 128:(j + 1) * 128],
                                    rhs=w[:, kc, ct * 512:(ct + 1) * 512], start=(kc == 0), stop=(kc == 7)),
                                    reads=srcreads + [wb_], writes=[pob])
                            p.op("dve", lambda e, po=po, j=j, ct=ct, x=x: e.tensor_tensor(
                                out=x[:, j, ct * 512:(ct + 1) * 512], in0=po[:, :], in1=x[:, j, ct * 512:(ct + 1) * 512],
                                op=ALU.add), reads=[pob, xb], writes=[xb])

                proj_add(YT[i], ytreads, wout, woutb)
                if "hmix" in self.debug:
                    if tt == 0:
                        self.T["hmix"] = self.dram_tmp("hmix", [S, D], F32)
                        self.T["xo"] = self.dram_tmp("xo", [S, D], BF16)
                    p.dma("sp", self.T["hmix"][tt * TT:(tt + 1) * TT, :].rearrange("(j p) d -> p j d", p=128), x[:, :, :], reads=[xb])
                self.rms_cast(x, xb, xn, xnb, 4, junk, junkb, ss, ssb, rstd, rstdb)
                self.transpose_to(xn, xnb, xT, xTb, 4, 8, pst, pstb, self.ident, self.identb)
                for ct in range(8):
                    po, pob = pso[k % 4], psob[k % 4]
                    k += 1
                    for kc in range(8):
                        p.op("pe", lambda e, po=po, kc=kc, ct=ct: e.matmul(
                            out=po[:, :], lhsT=wq[:, kc, ct * 128:(ct + 1) * 128], rhs=xT[:, kc, :],
                            start=(kc == 0), stop=(kc == 7)), reads=[wqb, xTb], writes=[pob])
                    if ev % 2 == 0:
                        p.op("act", lambda e, po=po, ct=ct: e.activation(out=qT[:, ct, :], in_=po[:, :], func=AF.Copy,
                                                                         scale=1.0 / 16), reads=[pob], writes=[qTb])
                    else:
                        p.op("dve", lambda e, po=po, ct=ct: e.tensor_scalar(out=qT[:, ct, :], in0=po[:, :], scalar1=1.0 / 16,
                                                                            scalar2=None, op0=ALU.mult),
                             reads=[pob], writes=[qTb])
                    ev += 1
                for hh in range(4):
                    for mc in range(2):
                        po, pob = pso[k % 4], psob[k % 4]
                        k += 1
                        for dc in range(2):
                            p.op("pe", lambda e, po=po, hh=hh, mc=mc, dc=dc: e.matmul(
                                out=po[:, :], lhsT=kT[:, hh * 2 + dc, mc * 128:(mc + 1) * 128], rhs=qT[:, hh * 2 + dc, :],
                                start=(dc == 0), stop=(dc == 1)), reads=[kTb, qTb], writes=[pob])
                        p.op("act", lambda e, po=po, hh=hh, mc=mc: e.activation(out=PTm[:, hh, mc, :], in_=po[:, :],
                                                                                func=AF.Exp),
                             reads=[pob], writes=[PTmb[hh][mc]])
                    for j in range(4):
                        po, pob = pos[ko % 2], posb[ko % 2]
                        ko += 1
                        for mc in range(2):
                            p.op("pe", lambda e, po=po, hh=hh, mc=mc, j=j: e.matmul(
                                out=po[:, 0:257], lhsT=PTm[:, hh, mc, j * 128:(j + 1) * 128], rhs=vE[:, mc, hh, :],
                                start=(mc == 0), stop=(mc == 1)), reads=[PTmb[hh][mc], vEb], writes=[pob])
                        s_, sb_ = sm[ksm % 4], smb[ksm % 4]
                        ksm += 1
                        p.op("dve", lambda e, po=po, s_=s_: e.reciprocal(out=s_[:, 0:1], in_=po[:, 256:257]),
                             reads=[pob], writes=[sb_])
                        if ksm % 2 == 0:
                            p.op("act", lambda e, po=po, s_=s_, j=j, hh=hh: e.activation(
                                out=xn[:, j, hh * 256:(hh + 1) * 256], in_=po[:, 0:256], func=AF.Copy, scale=s_[:, 0:1]),
                                reads=[pob, sb_, xTb], writes=[xnb])
                        else:
                            p.op("dve", lambda e, po=po, s_=s_, j=j, hh=hh: e.tensor_scalar(
                                out=xn[:, j, hh * 256:(hh + 1) * 256], in0=po[:, 0:256], scalar1=s_[:, 0:1], scalar2=None,
                                op0=ALU.mult), reads=[pob, sb_, xTb], writes=[xnb])
                if "hmix" in self.debug:
                    p.dma("sp", self.T["xo"][tt * TT:(tt + 1) * TT, :].rearrange("(j p) d -> p j d", p=128), xn[:, :, :], reads=[xnb])
                self.transpose_to(xn, xnb, qT, qTb, 4, 8, pst, pstb, self.ident, self.identb)
                proj_add(qT, [qTb], wo, wob)
                p.dma("sp", T["h"][tt * TT:(tt + 1) * TT, :].rearrange("(j p) d -> p j d", p=128), x[:, :, :], reads=[xb])
            p.flush()

    def phase_mlp(self, layer):
        p, I, T, Tb = self.p, self.I, self.T, self.Tb
        last = (layer == DEPTH - 1)
        MT = 512
        NMT = S // MT
        HF = DFF // 2
        NFC = HF // 128
        with ExitStack() as st:
            if self.WA is not None:
                wup0 = self.WA[:, 0:16384].rearrange("p (k n) -> p k n", k=8)
                wdn0 = self.WA[:, 16384:32768].rearrange("p (k n) -> p k n", k=NFC)
            else:
                wup0 = self.sb(st, "wup", [128, 8, HF], BF16)
                wdn0 = self.sb(st, "wdn", [128, NFC, D], BF16)
            wup = [wup0, self.sb(st, "wup", [128, 8, HF], BF16)]
            wupb = [Buf("wup") for _ in range(2)]
            wdn = [wdn0, self.sb(st, "wdn", [128, NFC, D], BF16)]
            wdnb = [Buf("wdn") for _ in range(2)]
            gcol, gb = self.load_col(st, "gmlp", I["mlp_norm_g"][layer], 8)
            if self.WA is not None:
                gnx, gnxb = self.load_col(st, "gmixn", I["mix_norm_g"][layer + 1], 8)
            self._stg = [self.sb(st, "stg", [128, 1024], F32) for _ in range(2)]
            self._stgb = [Buf("stg%d" % i) for i in range(2)]
            self._stg_st = st

            def wgen(hf, engs=None):
                yield from self.load_w_gen(st, wup[hf], wupb[hf], I["w_up"][layer][:, hf * HF:(hf + 1) * HF], 8, HF,
                                           gcol=gcol, gbuf=gb, ncol_chunk=1024, engs=engs)
                yield from self.load_w_gen(st, wdn[hf], wdnb[hf], I["w_down"][layer][hf * HF:(hf + 1) * HF, :], NFC, D,
                                           ncol_chunk=1024, engs=engs)

            for _ in wgen(0):
                pass
            if last:
                gfin = self.sb(st, "gfin", [128, D], F32)
                gfinb = Buf("gfin")
                p.dma("sp", gfin[:, :], I["g_final"].partition_broadcast(128), writes=[gfinb])
            xh = [self.sb(st, "xh", [128, 2, D], F32) for _ in range(2)]
            xhb = [Buf("xh") for _ in range(2)]
            big = self.sb(st, "big", [128, NFC * MT], BF16)
            bigb = Buf("big")
            xn = big[:, 0:4 * D].rearrange("p (j d) -> p j d", j=4)
            hT = big[:, :].rearrange("p (f t) -> p f t", f=NFC)
            xT = self.sb(st, "xT", [128, 8, MT], BF16)
            xTb = Buf("xT")
            rl = [self.sb(st, "rl", [128, MT], F32) for _ in range(2)]
            rlb = [Buf("rl") for _ in range(2)]
            junk = self.sb(st, "junk", [128, D], BF16)
            junkb = Buf("junk", strict=True)
            ss = self.sb(st, "ss", [128, 4], F32)
            ssb = Buf("ss")
            rstd = self.sb(st, "rstd", [128, 4], F32)
            rstdb = Buf("rstd")
            pst = [self.ps(st, "pst", [128, 8, 128], BF16) for _ in range(2)]
            pstb = [Buf("pst") for _ in range(2)]
            pso = [self.ps(st, "pso", [128, 512], F32) for _ in range(6)]
            psob = [Buf("pso") for _ in range(6)]
            xnT = T["xnT"]

            def load_x(t, half):
                r0 = t * MT + half * 256
                p.dma("act", xh[half][:, :, :], T["h"][r0:r0 + 256, :].rearrange("(j p) d -> p j d", p=128),
                      writes=[xhb[half]])

            def load_xT(t):
                p.dma("sp", xT[:, :, :], xnT[:, t * MT:(t + 1) * MT].rearrange("(c p) t -> p c t", p=128), writes=[xTb])

            k = 0
            kr = 0
            for hf in range(2):
                pre = wgen(1, engs=("pool", "dve")) if hf == 0 else None
                if hf == 1 and self.WA is not None:
                    nodd = ((layer + 1) % 2 == 1)
                    NIN_ = 3 * D if nodd else EVEN_IN
                    self.win_pre = self.WA[:, 0:8 * NIN_].rearrange("p (k n) -> p k n", k=8)
                    pre = self.load_w_gen(st, self.win_pre, [wupb[0], wdnb[0]],
                                          I["w_in_odd"] if nodd else I["w_in_even"], 8, NIN_, gcol=gnx, gbuf=gnxb,
                                          ncol_chunk=1024, engs=("pool", "dve"))
                load_x(0, 0)
                load_x(0, 1)
                if hf == 1:
                    load_xT(0)
                for t in range(NMT):
                    if hf == 0:
                        for half in range(2):
                            x, xb = xh[half], xhb[half]
                            for jj in range(2):
                                p.op("act", lambda e, x=x, jj=jj, half=half: e.activation(
                                    out=xn[:, 2 * half + jj, :], in_=x[:, jj, :], func=AF.Square,
                                    accum_out=ss[:, 2 * half + jj:2 * half + jj + 1]), reads=[xb], writes=[bigb, ssb])
                        p.op("dve", lambda e: e.tensor_scalar(out=rstd[:, :], in0=ss[:, :], scalar1=1.0 / D, scalar2=EPS,
                                                              op0=ALU.mult, op1=ALU.add), reads=[ssb], writes=[rstdb])
                        p.op("act", lambda e: e.activation(out=rstd[:, :], in_=rstd[:, :], func=AF.Sqrt),
                             reads=[rstdb], writes=[rstdb])
                        p.op("dve", lambda e: e.reciprocal(out=rstd[:, :], in_=rstd[:, :]), reads=[rstdb], writes=[rstdb])
                        for half in range(2):
                            x, xb = xh[half], xhb[half]
                            for jj in range(2):
                                j = 2 * half + jj
                                p.op("dve", lambda e, x=x, jj=jj, j=j: e.tensor_scalar(
                                    out=xn[:, j, :], in0=x[:, jj, :], scalar1=rstd[:, j:j + 1], scalar2=None,
                                    op0=ALU.mult), reads=[xb, rstdb], writes=[bigb])
                        self.transpose_to(xn, bigb, xT, xTb, 4, 8, pst, pstb, self.ident, self.identb)
                        p.dma("sp", xnT[:, t * MT:(t + 1) * MT].rearrange("(c p) t -> p c t", p=128), xT[:, :, :],
                              reads=[xTb])
                    for fc in range(NFC):
                        po, pob = pso[k % 6], psob[k % 6]
                        k += 1
                        for kc in range(8):
                            p.op("pe", lambda e, po=po, kc=kc, fc=fc, hf=hf: e.matmul(
                                out=po[:, :], lhsT=wup[hf][:, kc, fc * 128:(fc + 1) * 128], rhs=xT[:, kc, :],
                                start=(kc == 0), stop=(kc == 7)), reads=[wupb[hf], xTb], writes=[pob])
                        r_, rb_ = rl[kr % 2], rlb[kr % 2]
                        kr += 1
                        p.op("act", lambda e, po=po, r_=r_: e.activation(out=r_[:, :], in_=po[:, :], func=AF.Relu),
                             reads=[pob], writes=[rb_])
                        eng = "dve" if fc % 2 == 0 else "pool"
                        p.op(eng, lambda e, r_=r_, fc=fc: e.tensor_tensor(out=hT[:, fc, :], in0=r_[:, :], in1=r_[:, :],
                                                                          op=ALU.mult), reads=[rb_], writes=[bigb])
                        if pre is not None and fc % 4 == 3:
                            next(pre, None)
                    if hf == 1 and t + 1 < NMT:
                        load_xT(t + 1)
                    for half in range(2):
                        x, xb = xh[half], xhb[half]
                        for jj in range(2):
                            j = 2 * half + jj
                            pos_ = []
                            for ct in range(2):
                                pos_.append((pso[k % 6], psob[k % 6]))
                                k += 1
                            for fc in range(NFC):
                                for ct in range(2):
                                    po, pob = pos_[ct]
                                    p.op("pe", lambda e, po=po, fc=fc, j=j, ct=ct, hf=hf: e.matmul(
                                        out=po[:, :], lhsT=hT[:, fc, j * 128:(j + 1) * 128],
                                        rhs=wdn[hf][:, fc, ct * 512:(ct + 1) * 512],
                                        start=(fc == 0), stop=(fc == NFC - 1)), reads=[bigb, wdnb[hf]], writes=[pob])
                            for ct in range(2):
                                po, pob = pos_[ct]
                                p.op("dve", lambda e, po=po, jj=jj, ct=ct, x=x: e.tensor_tensor(
                                    out=x[:, jj, ct * 512:(ct + 1) * 512], in0=po[:, :],
                                    in1=x[:, jj, ct * 512:(ct + 1) * 512], op=ALU.add), reads=[pob, xb], writes=[xb])
                        r0 = t * MT + half * 256
                        if not (last and hf == 1):
                            p.dma("sp", T["h"][r0:r0 + 256, :].rearrange("(j p) d -> p j d", p=128), x[:, :, :],
                                  reads=[xb])
                        else:
                            for jj in range(2):
                                p.op("act", lambda e, jj=jj, x=x, half=half: e.activation(
                                    out=junk[:, :], in_=x[:, jj, :], func=AF.Square,
                                    accum_out=ss[:, 2 * half + jj:2 * half + jj + 1]), reads=[xb], writes=[junkb, ssb])
                            c0 = 2 * half
                            p.op("dve", lambda e, c0=c0: e.tensor_scalar(
                                out=rstd[:, c0:c0 + 2], in0=ss[:, c0:c0 + 2], scalar1=1.0 / D, scalar2=EPS, op0=ALU.mult,
                                op1=ALU.add), reads=[ssb], writes=[rstdb])
                            p.op("act", lambda e, c0=c0: e.activation(out=rstd[:, c0:c0 + 2], in_=rstd[:, c0:c0 + 2],
                                                                      func=AF.Sqrt), reads=[rstdb], writes=[rstdb])
                            p.op("dve", lambda e, c0=c0: e.reciprocal(out=rstd[:, c0:c0 + 2], in_=rstd[:, c0:c0 + 2]),
                                 reads=[rstdb], writes=[rstdb])
                            for jj in range(2):
                                p.op("dve", lambda e, jj=jj, x=x, c0=c0: e.scalar_tensor_tensor(
                                    out=x[:, jj, :], in0=x[:, jj, :], scalar=rstd[:, c0 + jj:c0 + jj + 1], in1=gfin[:, :],
                                    op0=ALU.mult, op1=ALU.mult), reads=[xb, rstdb, gfinb], writes=[xb])
                            p.dma("sp", self.out[r0:r0 + 256, :].rearrange("(j p) d -> p j d", p=128), x[:, :, :],
                                  reads=[xb])
                        if t + 1 < NMT:
                            load_x(t + 1, half)
                if pre is not None:
                    for _ in pre:
                        pass
            p.flush()


def make_in_maps(inputs):
    f = lambda a: np.ascontiguousarray(np.asarray(a, dtype=np.float32))
    shared = {}
    for n in ("g_mem", "g_final", "mix_norm_g", "xattn_norm_g", "mlp_norm_g", "xattn_wq", "xattn_wkv",
              "xattn_wo", "w_up", "w_down"):
        shared[n] = f(inputs[n])
    for n in ("w_in_even", "conv_w", "conv_b", "w_rgate", "b_rgate", "w_igate", "b_igate", "lru_lambda",
              "fox_forget_b", "w_out_even", "w_in_odd", "lambda_q1", "lambda_k1", "lambda_q2", "lambda_k2",
              "diff_norm_g", "w_out_odd"):
        shared[n] = f(np.asarray(inputs[n])[0])
    zeros = {k: np.zeros_like(v) for k, v in shared.items()}
    zx = np.zeros((S, D), np.float32)
    zm = np.zeros((NMEM, D), np.float32)
    maps = []
    for c in range(N_CORES):
        if c in CORE_OF_BATCH:
            b = CORE_OF_BATCH.index(c)
            m = dict(shared)
            m["x"] = f(np.asarray(inputs["x"])[b])
            m["mem"] = f(np.asarray(inputs["mem"])[b])
        else:
            m = dict(zeros)
            m["x"] = zx
            m["mem"] = zm
        maps.append(m)
    return maps


_NC_CACHE = {}


def kernel(**inputs):
    if "nc" not in _NC_CACHE:
        _NC_CACHE["nc"] = Builder().build()
    nc = _NC_CACHE["nc"]
    maps = make_in_maps(inputs)
    res = run_bass_kernel_spmd(nc, maps, core_ids=list(range(N_CORES)))
    out = np.stack([np.asarray(res.results[CORE_OF_BATCH[b]]["out"], dtype=np.float32) for b in range(NB)], axis=0)
    return out
```

```python
import math
from contextlib import ExitStack

import numpy as np
import concourse.bass as bass
import concourse.mybir as mybir
from concourse.bass_utils import run_bass_kernel_spmd

F32 = mybir.dt.float32
BF16 = mybir.dt.bfloat16
AF = mybir.ActivationFunctionType
ALU = mybir.AluOpType
AX = mybir.AxisListType

D = 1024
S = 4096
NB = 4
NMEM = 256
EPS = 1e-6
DEPTH = 2
EVEN_IN = 2568
DFF = 4096
TT = 512
NTT = S // TT
NBLK = S // 128
N_CORES = 8
CORE_OF_BATCH = [0, 1, 4, 5]


class Buf:
    __slots__ = ("name", "w", "r", "strict")

    def __init__(self, name="", strict=False):
        self.name = name
        self.strict = strict
        self.w = []
        self.r = {}


class Op:
    __slots__ = ("eng", "fn", "reads", "writes", "dma", "need_inc", "ev", "phase", "waits", "sem_prev", "wadd")

    def __init__(self, eng, fn, reads, writes, dma, wadd=()):
        self.wadd = list(wadd)
        self.eng = eng
        self.fn = fn
        self.reads = reads
        self.writes = writes
        self.dma = dma
        self.need_inc = dma
        self.ev = None
        self.waits = None
        self.sem_prev = None


class Prog:
    ENG = ("pe", "act", "dve", "pool", "sp")
    EPOCH = 8000
    NDMA = 40

    def __init__(self, nc):
        self.nc = nc
        self.eo = {"pe": nc.tensor, "act": nc.scalar, "dve": nc.vector, "pool": nc.gpsimd, "sp": nc.sync}
        self.ops = []
        self.phase = 0
        self.stack = ExitStack()
        self.cnt = {e: 0 for e in self.ENG}
        self.esems = {e: [] for e in self.ENG}
        self.dsems = []
        self.dtot = [0] * self.NDMA
        self.dlast = [None] * self.NDMA
        self.dma_i = 0
        self.last = {e: None for e in self.ENG}
        self.waited = {e: {} for e in self.ENG}
        self.n_ins = 0

    def op(self, eng, fn, reads=(), writes=()):
        o = Op(eng, fn, list(reads), list(writes), False)
        self.ops.append(o)
        return o

    def dma(self, q, out, in_, reads=(), writes=(), wadd=()):
        o = Op(q, lambda e: e.dma_start(out=out, in_=in_), list(reads), list(writes), True, wadd)
        self.ops.append(o)
        return o

    def _esem(self, e, k):
        lst = self.esems[e]
        while len(lst) <= k:
            lst.append(self.stack.enter_context(self.nc.semaphore("s_%s_%d" % (e, len(lst)))))
        return lst[k]

    def _dsem(self, i):
        while len(self.dsems) <= i:
            self.dsems.append(self.stack.enter_context(self.nc.semaphore("s_dma_%d" % len(self.dsems))))
        return self.dsems[i]

    def flush(self):
        ops = self.ops
        self.ops = []
        ph = self.phase
        for o in ops:
            o.phase = ph
        for o in ops:
            deps = []
            for b in o.reads:
                for w in b.w:
                    if w.phase == ph:
                        deps.append((w, True))
            for b in o.writes:
                for w in b.w:
                    if w.phase == ph:
                        deps.append((w, b.strict))
                for r in b.r.values():
                    if r.phase == ph:
                        deps.append((r, False))
            for b in o.wadd:
                for r in b.r.values():
                    if r.phase == ph:
                        deps.append((r, False))
            real = []
            for d, raw in deps:
                if d is o:
                    continue
                if (not d.dma) and (not o.dma) and d.eng == o.eng and not raw:
                    continue
                real.append(d)
                d.need_inc = True
            o.waits = real
            for b in o.reads:
                key = ("dma", id(o)) if o.dma else o.eng
                b.r[key] = o
            for b in o.writes:
                b.w = [o]
                b.r = {}
            for b in o.wadd:
                b.w.append(o)
            if not o.dma:
                self.last[o.eng] = o
        for e in self.ENG:
            if self.last[e] is not None and self.last[e].phase == ph:
                self.last[e].need_inc = True
        for o in ops:
            if o.dma:
                s = self.dma_i % self.NDMA
                self.dma_i += 1
                o.sem_prev = self.dlast[s]
                self.dtot[s] += 16
                self.dlast[s] = o
                o.ev = (self._dsem(s), self.dtot[s])
            elif o.need_inc:
                self.cnt[o.eng] += 1
                c = self.cnt[o.eng] - 1
                o.ev = (self._esem(o.eng, c // self.EPOCH), c % self.EPOCH + 1)
        for o in ops:
            e = self.eo[o.eng]
            wt = self.waited[o.eng]
            lst = list(o.waits)
            if o.dma and o.sem_prev is not None:
                lst.append(o.sem_prev)
            for d in lst:
                sem, val = d.ev
                if wt.get(id(sem), 0) >= val:
                    continue
                e.wait_ge(sem, val)
                wt[id(sem)] = val
                self.n_ins += 1
            ins = o.fn(e)
            self.n_ins += 1
            if o.dma:
                ins.then_inc(o.ev[0], 16)
            elif o.need_inc:
                ins.then_inc(o.ev[0], 1)
        for e in self.ENG:
            eo = self.eo[e]
            wt = self.waited[e]
            for f in self.ENG:
                lo = self.last[f]
                if f == e or lo is None or lo.ev is None:
                    continue
                sem, val = lo.ev
                if wt.get(id(sem), 0) < val:
                    eo.wait_ge(sem, val)
                    wt[id(sem)] = val
            for s in range(len(self.dsems)):
                sem = self.dsems[s]
                if wt.get(id(sem), 0) < self.dtot[s]:
                    eo.wait_ge(sem, self.dtot[s])
                    wt[id(sem)] = self.dtot[s]
        self.phase += 1

    def close(self):
        self.stack.close()


class _Ticker:
    def __init__(self, p, gen, every):
        self.p, self.gen, self.every, self.n = p, gen, every, 0

    def _tick(self):
        self.n += 1
        if self.gen is not None and self.n % self.every == 0:
            if next(self.gen, "done") == "done":
                self.gen = None

    def op(self, *a, **k):
        r = self.p.op(*a, **k)
        self._tick()
        return r

    def dma(self, *a, **k):
        r = self.p.dma(*a, **k)
        self._tick()
        return r

    def drain(self):
        if self.gen is not None:
            for _ in self.gen:
                pass
            self.gen = None

    def flush(self):
        self.p.flush()


class Builder:
    def __init__(self, debug=(), stop=None):
        self.nc = bass.Bass("TRN2", target_bir_lowering=False)
        self.p = Prog(self.nc)
        self.debug = set(debug)
        self.stop = stop
        self.gst = ExitStack()
        self.uid = 0
        self.cast_i = 0

    def dram_in(self, name, shape, dt=F32):
        return self.nc.dram_tensor(name, list(shape), dt, kind="ExternalInput").ap()

    def dram_tmp(self, name, shape, dt):
        kind = "ExternalOutput" if name in self.debug else "Internal"
        return self.nc.dram_tensor(name, list(shape), dt, kind=kind).ap()

    def sb(self, st, name, shape, dt):
        self.uid += 1
        return st.enter_context(self.nc.sbuf_tensor("%s_%d" % (name, self.uid), list(shape), dt))

    def ps(self, st, name, shape, dt=F32):
        self.uid += 1
        return st.enter_context(self.nc.psum_tensor("%s_%d" % (name, self.uid), list(shape), dt))

    def load_w(self, *a, **k):
        for _ in self.load_w_gen(*a, **k):
            pass

    def load_w_gen(self, st, dst, dstbuf, src, KC, N, gcol=None, gbuf=None, mul2=1.0, ncol_chunk=2048, engs=None):
        p = self.p
        dstbufs = dstbuf if isinstance(dstbuf, list) else [dstbuf]
        if not hasattr(self, "_stg") or self._stg_st is not st:
            self._stg = [self.sb(st, "stg", [128, 2048], F32) for _ in range(3)]
            self._stgb = [Buf("stg%d" % i) for i in range(3)]
            self._stg_st = st
        for kc in range(KC):
            for c0 in range(0, N, ncol_chunk):
                cw = min(ncol_chunk, N - c0)
                i = self.cast_i % len(self._stg)
                self.cast_i += 1
                stg, sbf = self._stg[i], self._stgb[i]
                p.dma("sp", stg[:, 0:cw], src[kc * 128:(kc + 1) * 128, c0:c0 + cw], writes=[sbf])
                o = dst[:, kc, c0:c0 + cw]
                eng = (engs or ("pool", "dve", "act", "dve"))[self.cast_i % len(engs or (0, 1, 2, 3))]
                if gcol is not None:
                    g = gcol[:, kc:kc + 1]
                    if eng == "act":
                        if mul2 != 1.0:
                            eng = "dve"
                        else:
                            p.op("act", lambda e, o=o, s=stg[:, 0:cw], g=g: e.activation(
                                out=o, in_=s, func=AF.Copy, scale=g), reads=[sbf, gbuf], writes=dstbufs)
                    if eng != "act":
                        p.op(eng, lambda e, o=o, s=stg[:, 0:cw], g=g: e.tensor_scalar(
                            out=o, in0=s, scalar1=g, scalar2=float(mul2), op0=ALU.mult, op1=ALU.mult),
                            reads=[sbf, gbuf], writes=dstbufs)
                elif eng == "act":
                    p.op("act", lambda e, o=o, s=stg[:, 0:cw]: e.copy(out=o, in_=s), reads=[sbf], writes=dstbufs)
                else:
                    p.op(eng, lambda e, o=o, s=stg[:, 0:cw]: e.tensor_copy(out=o, in_=s),
                         reads=[sbf], writes=dstbufs)
                yield

    def load_col(self, st, name, src, KC):
        t = self.sb(st, name, [128, KC], F32)
        b = Buf(name)
        for kc in range(KC):
            self.p.dma("sp", t[:, kc:kc + 1], src[kc * 128:(kc + 1) * 128].rearrange("(p o) -> p o", o=1),
                       wadd=[b])
        return t, b

    def rms_cast(self, x, xb, xn, xnb, J, junk, junkb, ss, ssb, rstd, rstdb):
        p = self.p
        for j in range(J):
            p.op("act", lambda e, j=j: e.activation(out=junk[:, :], in_=x[:, j, :], func=AF.Square,
                                                    accum_out=ss[:, j:j + 1]),
                 reads=[xb], writes=[junkb, ssb])
        p.op("dve", lambda e: e.tensor_scalar(out=rstd[:, 0:J], in0=ss[:, 0:J], scalar1=1.0 / D, scalar2=EPS,
                                              op0=ALU.mult, op1=ALU.add), reads=[ssb], writes=[rstdb])
        p.op("act", lambda e: e.activation(out=rstd[:, 0:J], in_=rstd[:, 0:J], func=AF.Sqrt),
             reads=[rstdb], writes=[rstdb])
        p.op("dve", lambda e: e.reciprocal(out=rstd[:, 0:J], in_=rstd[:, 0:J]), reads=[rstdb], writes=[rstdb])
        for j in range(J):
            p.op("dve", lambda e, j=j: e.tensor_scalar(out=xn[:, j, :], in0=x[:, j, :], scalar1=rstd[:, j:j + 1],
                                                       scalar2=None, op0=ALU.mult),
                 reads=[xb, rstdb], writes=[xnb])

    def transpose_to(self, src, srcb, dst, dstb, J, C, pst, pstb, ident, identb, evac_eng=("act", "dve")):
        p = self.p
        for j in range(J):
            k = j % len(pst)
            pt, ptb = pst[k], pstb[k]
            for c in range(C):
                p.op("pe", lambda e, j=j, c=c, pt=pt: e.transpose(out=pt[:, c, :], in_=src[:, j, c * 128:(c + 1) * 128],
                                                                 identity=ident[:, :]),
                     reads=[srcb, identb], writes=[ptb])
            eng = evac_eng[j % len(evac_eng)]
            if eng == "act":
                p.op("act", lambda e, j=j, pt=pt: e.copy(out=dst[:, 0:C, j * 128:(j + 1) * 128], in_=pt[:, 0:C, :]),
                     reads=[ptb], writes=[dstb])
            else:
                p.op(eng, lambda e, j=j, pt=pt: e.tensor_copy(out=dst[:, 0:C, j * 128:(j + 1) * 128], in_=pt[:, 0:C, :]),
                     reads=[ptb], writes=[dstb])

    def build(self):
        nc, p = self.nc, self.p
        I = {}
        I["x"] = self.dram_in("x", [S, D])
        I["mem"] = self.dram_in("mem", [NMEM, D])
        for n in ("g_mem", "g_final"):
            I[n] = self.dram_in(n, [D])
        for n in ("mix_norm_g", "xattn_norm_g", "mlp_norm_g"):
            I[n] = self.dram_in(n, [DEPTH, D])
        I["w_in_even"] = self.dram_in("w_in_even", [D, EVEN_IN])
        I["conv_w"] = self.dram_in("conv_w", [4, 512])
        for n in ("conv_b", "b_rgate", "b_igate", "lru_lambda"):
            I[n] = self.dram_in(n, [512])
        I["w_rgate"] = self.dram_in("w_rgate", [8, 64, 64])
        I["w_igate"] = self.dram_in("w_igate", [8, 64, 64])
        I["fox_forget_b"] = self.dram_in("fox_forget_b", [8])
        I["w_out_even"] = self.dram_in("w_out_even", [D, D])
        I["w_in_odd"] = self.dram_in("w_in_odd", [D, 3 * D])
        for n in ("lambda_q1", "lambda_k1", "lambda_q2", "lambda_k2"):
            I[n] = self.dram_in(n, [64])
        I["diff_norm_g"] = self.dram_in("diff_norm_g", [128])
        I["w_out_odd"] = self.dram_in("w_out_odd", [D, D])
        I["xattn_wq"] = self.dram_in("xattn_wq", [DEPTH, D, D])
        I["xattn_wkv"] = self.dram_in("xattn_wkv", [DEPTH, D, 2 * D])
        I["xattn_wo"] = self.dram_in("xattn_wo", [DEPTH, D, D])
        I["w_up"] = self.dram_in("w_up", [DEPTH, D, DFF])
        I["w_down"] = self.dram_in("w_down", [DEPTH, DFF, D])
        self.I = I
        self.out = nc.dram_tensor("out", [S, D], F32, kind="ExternalOutput").ap()

        T = {}
        T["h"] = self.dram_tmp("h", [S, D], F32)
        T["xgT"] = self.dram_tmp("xgT", [1024, S], F32)
        T["qT"] = self.dram_tmp("qT", [1024, S], BF16)
        T["kT"] = self.dram_tmp("kT", [1024, S], BF16)
        T["fT"] = self.dram_tmp("fT", [8, S], F32)
        T["vE0"] = self.dram_tmp("vE0", [S, 8, 65], BF16)
        T["vE1"] = self.dram_tmp("vE1", [S, 8, 129], BF16)
        T["ylruT"] = self.dram_tmp("ylruT", [512, S], BF16)
        T["y"] = self.dram_tmp("y", [S, D], BF16)
        T["xnT"] = self.dram_tmp("xnT", [D, S], BF16)
        self.T = T
        self.Tb = {k: Buf(k) for k in T}

        g = self.gst
        self.ident = self.sb(g, "ident", [128, 128], BF16)
        self.identb = Buf("ident")
        self.identf = self.sb(g, "identf", [128, 128], F32)
        self.identfb = Buf("identf")
        self.tri = self.sb(g, "tri", [128, 128], BF16)
        self.trib = Buf("tri")
        self.sel = self.sb(g, "sel", [128, 128], F32)
        self.selb = Buf("sel")
        self.kmemT = [self.sb(g, "kmemT", [128, 8, NMEM], BF16) for _ in range(DEPTH)]
        self.kmemTb = [Buf("kmemT") for _ in range(DEPTH)]
        self.vmemE = [self.sb(g, "vmemE", [128, 2, 4, 257], BF16) for _ in range(DEPTH)]
        self.vmemEb = [Buf("vmemE") for _ in range(DEPTH)]
        self.consts()
        self.fuse_mem = (self.stop != "mem")
        if not self.fuse_mem:
            self.phase_mem()
            return self.finish()
        self.WA = None
        self.win_pre = None
        self._nxt = None
        for layer in range(DEPTH):
            self.phase_inproj(layer)
            if self._nxt is not None:
                self._nxt.close()
                self._nxt = None
                self.WA = None
                self.win_pre = None
            if self.stop == "inproj%d" % layer:
                return self.finish()
            if layer % 2 == 0:
                self.phase_lru()
                if self.stop == "lru":
                    return self.finish()
            with ExitStack() as lst:
                self.midw = self.midw_alloc(lst, layer)
                self.phase_attn(layer)
                if self.stop == "attn%d" % layer:
                    return self.finish()
                self.phase_mid(layer)
                self.midw = None
            if self.stop == "mid%d" % layer:
                return self.finish()
            if layer + 1 < DEPTH:
                self._nxt = ExitStack()
                self.WA = self.sb(self._nxt, "WA", [128, 32768], BF16)
            self.phase_mlp(layer)
            if self.stop == "mlp%d" % layer:
                return self.finish()
        return self.finish()

    def finish(self):
        self.p.flush()
        self.gst.close()
        self.p.close()
        return self.nc

    def consts(self):
        p = self.p
        ident, identf, tri, sel = self.ident, self.identf, self.tri, self.sel
        with ExitStack() as st:
            tmp = self.sb(st, "ctmp", [128, 128], F32)
            tb = Buf("ctmp")
            p.op("pool", lambda e: e.memset(tmp[:, :], 1.0), writes=[tb])
            p.op("pool", lambda e: e.affine_select(out=identf[:, :], in_=tmp[:, :], pattern=[[-1, 128]],
                                                    compare_op=ALU.is_equal, fill=0.0, base=0, channel_multiplier=1),
                 reads=[tb], writes=[self.identfb])
            p.op("pool", lambda e: e.tensor_copy(out=ident[:, :], in_=identf[:, :]), reads=[self.identfb],
                 writes=[self.identb])
            p.op("pool", lambda e: e.affine_select(out=tri[:, :], in_=tmp[:, :], pattern=[[1, 128]],
                                                    compare_op=ALU.is_ge, fill=0.0, base=0, channel_multiplier=-1),
                 reads=[tb], writes=[self.trib])
            p.op("pool", lambda e: e.affine_select(out=sel[:, :], in_=tmp[:, :], pattern=[[0, 128]],
                                                    compare_op=ALU.is_equal, fill=0.0, base=-127, channel_multiplier=1),
                 reads=[tb], writes=[self.selb])
            p.flush()

    def phase_mem(self):
        with ExitStack() as st:
            for _ in self.mem_gen(st):
                pass
            self.p.flush()

    def mem_gen(self, st, npso=4, nstg=3):
        p, I = self.p, self.I
        x = self.sb(st, "memx", [128, 2, D], F32)
        xb = Buf("memx")
        xn = self.sb(st, "memn", [128, 2, D], BF16)
        xnb = Buf("memn")
        xT = self.sb(st, "memT", [128, 8, NMEM], BF16)
        xTb = Buf("memT")
        junk = self.sb(st, "junk", [128, D], BF16)
        junkb = Buf("junk", strict=True)
        ss = self.sb(st, "ss", [128, 4], F32)
        ssb = Buf("ss")
        rstd = self.sb(st, "rstd", [128, 4], F32)
        rstdb = Buf("rstd")
        pst = [self.ps(st, "pst", [128, 8, 128], BF16) for _ in range(2)]
        pstb = [Buf("pst") for _ in range(2)]
        pso = [self.ps(st, "pso", [128, 512], F32) for _ in range(npso)]
        psob = [Buf("pso") for _ in range(npso)]
        gcol, gb = self.load_col(st, "gmem", I["g_mem"], 8)
        wkv = self.sb(st, "wkv", [128, 8, 2 * D], BF16)
        self._stg = [self.sb(st, "stg", [128, 2048], F32) for _ in range(nstg)]
        self._stgb = [Buf("stg%d" % i) for i in range(nstg)]
        self._stg_st = st
        p.dma("sp", x[:, :, :], I["mem"].rearrange("(j p) d -> p j d", p=128), writes=[xb])
        yield
        self.rms_cast(x, xb, xn, xnb, 2, junk, junkb, ss, ssb, rstd, rstdb)
        yield
        self.transpose_to(xn, xnb, xT, xTb, 2, 8, pst, pstb, self.ident, self.identb)
        yield
        k = 0
        wb = Buf("wkv")
        for l in range(DEPTH):
            yield from self.load_w_gen(st, wkv, wb, I["xattn_wkv"][l], 8, 2 * D, gcol=gcol, gbuf=gb)
            kT, kTb = self.kmemT[l], self.kmemTb[l]
            vE, vEb = self.vmemE[l], self.vmemEb[l]
            p.op("pool", lambda e, vE=vE: e.memset(vE[:, :, :, 256:257], 1.0), writes=[vEb])
            for ct in range(8):
                po, pob = pso[k % npso], psob[k % npso]
                k += 1
                for kc in range(8):
                    p.op("pe", lambda e, po=po, kc=kc, ct=ct: e.matmul(
                        out=po[:, 0:NMEM], lhsT=wkv[:, kc, ct * 128:(ct + 1) * 128], rhs=xT[:, kc, :],
                        start=(kc == 0), stop=(kc == 7)), reads=[wb, xTb], writes=[pob])
                p.op("act", lambda e, po=po, ct=ct, kT=kT: e.copy(out=kT[:, ct, :], in_=po[:, 0:NMEM]),
                     reads=[pob], writes=[kTb])
                yield
            for mc in range(2):
                for c2 in range(2):
                    po, pob = pso[k % npso], psob[k % npso]
                    k += 1
                    for kc in range(8):
                        p.op("pe", lambda e, po=po, kc=kc, mc=mc, c2=c2: e.matmul(
                            out=po[:, :], lhsT=xT[:, kc, mc * 128:(mc + 1) * 128],
                            rhs=wkv[:, kc, D + c2 * 512:D + (c2 + 1) * 512],
                            start=(kc == 0), stop=(kc == 7)), reads=[wb, xTb], writes=[pob])
                    p.op("dve", lambda e, po=po, mc=mc, c2=c2, vE=vE: e.tensor_copy(
                        out=vE[:, mc, 2 * c2:2 * c2 + 2, 0:256],
                        in_=po[:, :].rearrange("p (h d) -> p h d", h=2)),
                        reads=[pob], writes=[vEb])
                    yield

    def phase_inproj(self, layer):
        p, I, T, Tb = self.p, self.I, self.T, self.Tb
        even = (layer % 2 == 0)
        NIN = EVEN_IN if even else 3 * D
        hsrc = I["x"] if layer == 0 else T["h"]
        with ExitStack() as st:
            winb = Buf("win")
            if self.win_pre is not None:
                win = self.win_pre
            else:
                gcol, gb = self.load_col(st, "gmix", I["mix_norm_g"][layer], 8)
                win = self.sb(st, "win", [128, 8, NIN], BF16)
                self.load_w(st, win, winb, I["w_in_even"] if even else I["w_in_odd"], 8, NIN, gcol=gcol, gbuf=gb,
                            ncol_chunk=1536)
            xs = [self.sb(st, "x", [128, 4, D], F32) for _ in range(2)]
            xsb = [Buf("x") for _ in range(2)]
            xn = self.sb(st, "xn", [128, 4, D], BF16)
            xnb = Buf("xn")
            xTs = [self.sb(st, "xT", [128, 8, TT], BF16) for _ in range(2)]
            xTsb = [Buf("xT") for _ in range(2)]
            junk = self.sb(st, "junk", [128, D], BF16)
            junkb = Buf("junk", strict=True)
            ss = self.sb(st, "ss", [128, 4], F32)
            ssb = Buf("ss")
            rstd = self.sb(st, "rstd", [128, 4], F32)
            rstdb = Buf("rstd")
            pst = [self.ps(st, "pst", [128, 8, 128], BF16) for _ in range(2)]
            pstb = [Buf("pst") for _ in range(2)]
            pso = [self.ps(st, "pso", [128, 512], F32) for _ in range(6)]
            psob = [Buf("pso") for _ in range(6)]
            NO = 4
            of32 = [self.sb(st, "of32", [128, TT], F32) for _ in range(NO)]
            of32b = [Buf("of32") for _ in range(NO)]
            obf = [self.sb(st, "obf", [128, TT], BF16) for _ in range(NO)]
            obfb = [Buf("obf") for _ in range(NO)]
            dv = 64 if even else 128
            vts = [self.sb(st, "vt", [128, 8, dv + 1], BF16) for _ in range(2)]
            vtsb = [Buf("vt") for _ in range(2)]
            for v_ in vts:
                p.op("pool", lambda e, v_=v_: e.memset(v_[:, :, dv:dv + 1], 1.0), writes=[vtsb[vts.index(v_)]])
            vE = T["vE0"] if even else T["vE1"]
            vEb = Tb["vE0"] if even else Tb["vE1"]
            k = 0
            io = 0
            iv = 0
            ev = 0
            def load(tt):
                p.dma("sp", xs[tt % 2][:, :, :], hsrc[tt * TT:(tt + 1) * TT, :].rearrange("(j p) d -> p j d", p=128),
                      writes=[xsb[tt % 2]])

            load(0)
            for tt in range(NTT):
                x, xb = xs[tt % 2], xsb[tt % 2]
                xT, xTb = xTs[tt % 2], xTsb[tt % 2]
                if tt + 1 < NTT:
                    load(tt + 1)
                self.rms_cast(x, xb, xn, xnb, 4, junk, junkb, ss, ssb, rstd, rstdb)
                self.transpose_to(xn, xnb, xT, xTb, 4, 8, pst, pstb, self.ident, self.identb)
                if even:
                    tiles = [(c * 128, 128, 0, c * 128) for c in range(8)]
                    tiles += [(1024 + c * 128, 128, 1, c * 128) for c in range(4)]
                    tiles += [(1536 + c * 128, 128, 2, c * 128) for c in range(4)]
                    tiles += [(2560, 8, 3, 0)]
                    vcol = 2048
                    nvt = 1
                else:
                    tiles = [(c * 128, 128, 1, c * 128) for c in range(8)]
                    tiles += [(1024 + c * 128, 128, 2, c * 128) for c in range(8)]
                    vcol = 2048
                    nvt = 2
                for (c0, ncol, kind, r0) in tiles:
                    po, pob = pso[k % 6], psob[k % 6]
                    k += 1
                    for kc in range(8):
                        p.op("pe", lambda e, po=po, kc=kc, c0=c0, ncol=ncol, xT=xT: e.matmul(
                            out=po[0:ncol, :], lhsT=win[:, kc, c0:c0 + ncol], rhs=xT[:, kc, :],
                            start=(kc == 0), stop=(kc == 7)), reads=[winb, xTb], writes=[pob])
                    eng = ("act", "dve")[ev % 2]
                    ev += 1
                    if kind in (0, 3):
                        o, ob = of32[io % NO], of32b[io % NO]
                    else:
                        o, ob = obf[io % NO], obfb[io % NO]
                    io += 1
                    scale = 0.125 if kind == 1 else 1.0
                    if eng == "act":
                        p.op("act", lambda e, o=o, po=po, ncol=ncol, scale=scale: e.activation(
                            out=o[0:ncol, :], in_=po[0:ncol, :], func=AF.Copy, scale=scale),
                            reads=[pob], writes=[ob])
                    else:
                        p.op("dve", lambda e, o=o, po=po, ncol=ncol, scale=scale: e.tensor_scalar(
                            out=o[0:ncol, :], in0=po[0:ncol, :], scalar1=scale, scalar2=None, op0=ALU.mult),
                            reads=[pob], writes=[ob])
                    if kind == 0:
                        dst, dstb = T["xgT"], Tb["xgT"]
                    elif kind == 1:
                        dst, dstb = T["qT"], Tb["qT"]
                    elif kind == 2:
                        dst, dstb = T["kT"], Tb["kT"]
                    else:
                        dst, dstb = T["fT"], Tb["fT"]
                    p.dma("act" if False else "sp", dst[r0:r0 + ncol, tt * TT:(tt + 1) * TT], o[0:ncol, :], reads=[ob])
                for j in range(4):
                    vt, vtb = vts[iv % 2], vtsb[iv % 2]
                    iv += 1
                    for vc in range(nvt):
                        po, pob = pso[k % 6], psob[k % 6]
                        k += 1
                        for kc in range(8):
                            p.op("pe", lambda e, po=po, kc=kc, j=j, vc=vc, xT=xT: e.matmul(
                                out=po[:, :], lhsT=xT[:, kc, j * 128:(j + 1) * 128],
                                rhs=win[:, kc, vcol + vc * 512:vcol + (vc + 1) * 512],
                                start=(kc == 0), stop=(kc == 7)), reads=[winb, xTb], writes=[pob])
                        nh = 512 // dv
                        eng = ("act", "dve")[ev % 2]
                        ev += 1
                        if eng == "act":
                            p.op("act", lambda e, po=po, vt=vt, vc=vc, nh=nh: e.copy(
                                out=vt[:, vc * nh:(vc + 1) * nh, 0:dv], in_=po[:, :].rearrange("p (h d) -> p h d", h=nh)),
                                reads=[pob], writes=[vtb])
                        else:
                            p.op("dve", lambda e, po=po, vt=vt, vc=vc, nh=nh: e.tensor_copy(
                                out=vt[:, vc * nh:(vc + 1) * nh, 0:dv], in_=po[:, :].rearrange("p (h d) -> p h d", h=nh)),
                                reads=[pob], writes=[vtb])
                    r = tt * TT + j * 128
                    p.dma("sp", vE[r:r + 128, :, :], vt[:, :, :], reads=[vtb])
            p.flush()

    def phase_lru(self):
        p, I, T, Tb = self.p, self.I, self.T, self.Tb
        with ExitStack() as st:
            cw = self.sb(st, "cw", [128, 4, 4], F32)
            cwb = Buf("cw")
            for c in range(4):
                for k in range(4):
                    p.dma("sp", cw[:, c, k:k + 1],
                          I["conv_w"][k, c * 128:(c + 1) * 128].rearrange("(p o) -> p o", o=1), writes=[Buf()])
            cb, cbb = self.load_col(st, "cb", I["conv_b"], 4)
            br, brb = self.load_col(st, "br", I["b_rgate"], 4)
            bi, bib = self.load_col(st, "bi", I["b_igate"], 4)
            lam, lamb = self.load_col(st, "lam", I["lru_lambda"], 4)
            ls8 = self.sb(st, "ls8", [128, 4], F32)
            ls16 = self.sb(st, "ls16", [128, 4], F32)
            lsb = Buf("ls")
            p.op("act", lambda e: e.activation(out=ls8[:, :], in_=lam[:, :], func=AF.Exp, scale=-1.0),
                 reads=[lamb], writes=[lsb])
            p.op("act", lambda e: e.activation(out=ls8[:, :], in_=ls8[:, :], func=AF.Ln, bias=1.0),
                 reads=[lsb], writes=[lsb])
            p.op("dve", lambda e: e.tensor_scalar(out=ls16[:, :], in0=ls8[:, :], scalar1=-16.0, scalar2=None,
                                                  op0=ALU.mult), reads=[lsb], writes=[lsb])
            p.op("dve", lambda e: e.tensor_scalar(out=ls8[:, :], in0=ls8[:, :], scalar1=-8.0, scalar2=None,
                                                  op0=ALU.mult), reads=[lsb], writes=[lsb])
            wst = self.sb(st, "wst", [128, 2, 4, 128], F32)
            wstb = Buf("wst")
            wg = self.sb(st, "wg", [128, 2, 4, 128], BF16)
            wgb = Buf("wg")
            p.op("pool", lambda e: e.memset(wst[:, :, :, :], 0.0), writes=[wstb])
            p.flush()
            for gi, nm in enumerate(("w_rgate", "w_igate")):
                for c in range(4):
                    for s2 in range(2):
                        p.dma("sp", wst[s2 * 64:(s2 + 1) * 64, gi, c, s2 * 64:(s2 + 1) * 64], I[nm][2 * c + s2],
                              writes=[Buf()])
            p.flush()
            p.op("pool", lambda e: e.tensor_copy(out=wg[:, :, :, :], in_=wst[:, :, :, :]), writes=[wgb])
            bufs = [self.sb(st, "L%d" % i, [128, S], F32) for i in range(6)]
            bb = [Buf("L%d" % i) for i in range(6)]
            X, G, XC, R, Ig, A = bufs
            Xb, Gb, XCb, Rb, Ib, Ab = bb
            xcb16 = self.sb(st, "xcb16", [128, S], BF16)
            xcb16b = Buf("xcb16")
            yb16 = self.sb(st, "yb16", [128, S], BF16)
            yb16b = Buf("yb16")
            NPG = 2 if self.fuse_mem else 4
            pss = [self.ps(st, "psg", [128, 512], F32) for _ in range(NPG)]
            pssb = [Buf("psg") for _ in range(NPG)]
            if self.fuse_mem:
                p = _Ticker(self.p, self.mem_gen(st, npso=3, nstg=2), 2)
            k = 0
            for c in range(4):
                p.dma("sp", X[:, :], T["xgT"][c * 128:(c + 1) * 128, :], writes=[Xb])
                p.dma("sp", G[:, :], T["xgT"][512 + c * 128:512 + (c + 1) * 128, :], writes=[Gb])
                p.op("dve", lambda e, c=c: e.tensor_scalar(out=XC[:, :], in0=X[:, :], scalar1=cw[:, c, 3:4],
                                                           scalar2=cb[:, c:c + 1], op0=ALU.mult, op1=ALU.add),
                     reads=[Xb, cbb], writes=[XCb])
                for kk in range(3):
                    s = 3 - kk
                    p.op("dve", lambda e, c=c, kk=kk, s=s: e.scalar_tensor_tensor(
                        out=XC[:, s:S], in0=X[:, 0:S - s], scalar=cw[:, c, kk:kk + 1], in1=XC[:, s:S],
                        op0=ALU.mult, op1=ALU.add), reads=[Xb, XCb], writes=[XCb])
                p.op("act", lambda e: e.copy(out=xcb16[:, :], in_=XC[:, :]), reads=[XCb], writes=[xcb16b])
                for tt in range(NTT):
                    for gi in range(2):
                        ps_, psb_ = pss[k % NPG], pssb[k % NPG]
                        k += 1
                        dst, dstb = (R, Rb) if gi == 0 else (Ig, Ib)
                        bias = br if gi == 0 else bi
                        p.op("pe", lambda e, ps_=ps_, gi=gi, c=c, tt=tt: e.matmul(
                            out=ps_[:, :], lhsT=wg[:, gi, c, :], rhs=xcb16[:, tt * TT:(tt + 1) * TT],
                            start=True, stop=True), reads=[wgb, xcb16b], writes=[psb_])
                        p.op("act", lambda e, ps_=ps_, dst=dst, bias=bias, c=c, tt=tt: e.activation(
                            out=dst[:, tt * TT:(tt + 1) * TT], in_=ps_[:, :], func=AF.Sigmoid,
                            bias=bias[:, c:c + 1]), reads=[psb_, brb, bib], writes=[dstb])
                p.op("act", lambda e, c=c: e.activation(out=A[:, :], in_=R[:, :], func=AF.Exp, scale=ls8[:, c:c + 1]),
                     reads=[Rb, lsb], writes=[Ab])
                p.op("act", lambda e, c=c: e.activation(out=X[:, :], in_=R[:, :], func=AF.Exp, scale=ls16[:, c:c + 1]),
                     reads=[Rb, lsb], writes=[Xb])
                p.op("act", lambda e: e.activation(out=X[:, :], in_=X[:, :], func=AF.Sqrt, scale=-1.0, bias=1.0),
                     reads=[Xb], writes=[Xb])
                p.op("pool", lambda e: e.tensor_tensor(out=Ig[:, :], in0=Ig[:, :], in1=XC[:, :], op=ALU.mult),
                     reads=[Ib, XCb], writes=[Ib])
                p.op("dve", lambda e: e.tensor_tensor(out=Ig[:, :], in0=Ig[:, :], in1=X[:, :], op=ALU.mult),
                     reads=[Ib, Xb], writes=[Ib])
                p.op("dve", lambda e: e.tensor_tensor_scan(out=XC[:, :], data0=A[:, :], data1=Ig[:, :], initial=0.0,
                                                           op0=ALU.mult, op1=ALU.add),
                     reads=[Ab, Ib], writes=[XCb])
                p.op("pool", lambda e: e.tensor_tensor(out=R[:, :], in0=G[:, :], in1=G[:, :], op=ALU.mult),
                     reads=[Gb], writes=[Rb])
                p.op("pool", lambda e: e.tensor_scalar(out=R[:, :], in0=R[:, :], scalar1=0.044715, scalar2=1.0,
                                                       op0=ALU.mult, op1=ALU.add), reads=[Rb], writes=[Rb])
                p.op("pool", lambda e: e.tensor_tensor(out=R[:, :], in0=R[:, :], in1=G[:, :], op=ALU.mult),
                     reads=[Rb, Gb], writes=[Rb])
                p.op("act", lambda e: e.activation(out=R[:, :], in_=R[:, :], func=AF.Sigmoid,
                                                   scale=2.0 * math.sqrt(2.0 / math.pi)), reads=[Rb], writes=[Rb])
                p.op("dve", lambda e: e.tensor_tensor(out=XC[:, :], in0=XC[:, :], in1=G[:, :], op=ALU.mult),
                     reads=[XCb, Gb], writes=[XCb])
                p.op("dve", lambda e: e.tensor_tensor(out=yb16[:, :], in0=XC[:, :], in1=R[:, :], op=ALU.mult),
                     reads=[XCb, Rb], writes=[yb16b])
                p.dma("sp", T["ylruT"][c * 128:(c + 1) * 128, :], yb16[:, :], reads=[yb16b])
            if self.fuse_mem:
                p.drain()
            p.flush()

    def phase_attn(self, layer):
        p, I, T, Tb = self.p, self.I, self.T, self.Tb
        even = (layer % 2 == 0)
        dv = 64 if even else 128
        nmaps = 1 if even else 2
        qrows = 64 if even else 128
        vE = T["vE0"] if even else T["vE1"]
        lam_init = 0.8 - 0.6 * math.exp(-0.3 * layer)
        with ExitStack() as st:
            pss = [self.ps(st, "pss", [128, 512], F32) for _ in range(4)]
            pssb = [Buf("pss") for _ in range(4)]
            pos = [self.ps(st, "pos", [128, 512], F32) for _ in range(4)]
            posb = [Buf("pos") for _ in range(4)]
            if even:
                biasAll = self.sb(st, "biasAll", [128, 8, 8, 32], F32)
                biasb = Buf("biasAll")
                with ExitStack() as s2:
                    f = self.sb(s2, "f", [8, S], F32)
                    fb = Buf("f")
                    ones = self.sb(s2, "ones", [8, S], F32)
                    onesb = Buf("ones")
                    cp = self.sb(s2, "cp", [8, S], F32)
                    cpb = Buf("cp")
                    fbias = self.sb(s2, "fbias", [8, 1], F32)
                    fbiasb = Buf("fbias")
                    cpT = self.sb(s2, "cpT", [128, 32, 8], F32)
                    cpTb = Buf("cpT")
                    cpL = self.sb(s2, "cpL", [128, 32, 8], F32)
                    cpLb = Buf("cpL")
                    p.dma("sp", f[:, :], T["fT"][:, :], writes=[fb])
                    p.dma("sp", fbias[:, :], I["fox_forget_b"].rearrange("(p o) -> p o", o=1), writes=[fbiasb])
                    p.op("pool", lambda e: e.memset(ones[:, :], 1.0), writes=[onesb])
                    p.op("dve", lambda e: e.tensor_scalar(out=fbias[:, :], in0=fbias[:, :], scalar1=-1.0, scalar2=None,
                                                          op0=ALU.mult), reads=[fbiasb], writes=[fbiasb])
                    p.op("act", lambda e: e.activation(out=f[:, :], in_=f[:, :], func=AF.Exp, scale=-1.0,
                                                       bias=fbias[:, 0:1]), reads=[fb, fbiasb], writes=[fb])
                    p.op("act", lambda e: e.activation(out=f[:, :], in_=f[:, :], func=AF.Ln, bias=1.0),
                         reads=[fb], writes=[fb])
                    p.op("dve", lambda e: e.tensor_tensor_scan(out=cp[:, :], data0=ones[:, :], data1=f[:, :],
                                                               initial=0.0, op0=ALU.mult, op1=ALU.add),
                         reads=[fb, onesb], writes=[cpb])
                    pc = pss[0]
                    pcv = pc[:, 0:256].rearrange("p (j h) -> p j h", h=8)
                    for j in range(NBLK):
                        p.op("pe", lambda e, j=j: e.transpose(out=pcv[:, j, :], in_=cp[0:8, j * 128:(j + 1) * 128],
                                                              identity=self.identf[0:8, 0:8]),
                             reads=[cpb, self.identfb], writes=[pssb[0]])
                    p.op("act", lambda e: e.copy(out=cpT[:, :, :], in_=pcv), reads=[pssb[0]], writes=[cpTb])
                    p.op("pe", lambda e: e.matmul(out=pss[1][:, 0:256], lhsT=self.sel[:, :],
                                                  rhs=cpT[:, :, :].rearrange("p j h -> p (j h)"), start=True, stop=True),
                         reads=[cpTb, self.selb], writes=[pssb[1]])
                    p.op("act", lambda e: e.copy(out=cpL[:, :, :].rearrange("p j h -> p (j h)"), in_=pss[1][:, 0:256]),
                         reads=[pssb[1]], writes=[cpLb])
                    for h in range(8):
                        for qt in range(NTT):
                            p.op("dve", lambda e, h=h, qt=qt: e.tensor_scalar(
                                out=biasAll[:, h, qt, :], in0=cpT[:, :, h], scalar1=cpL[:, 4 * qt + 3, h:h + 1],
                                scalar2=None, op0=ALU.subtract), reads=[cpTb, cpLb], writes=[biasb])
                    p.flush()
            else:
                nlam = self.sb(st, "nlam", [128, 1], F32)
                nlamb = Buf("nlam")
                with ExitStack() as s2:
                    lt = self.sb(s2, "lt", [128, 4, 64], F32)
                    ltb = Buf("lt")
                    ssum = self.sb(s2, "ssum", [128, 2], F32)
                    ssumb = Buf("ssum")
                    for i, nm in enumerate(("lambda_q1", "lambda_k1", "lambda_q2", "lambda_k2")):
                        p.dma("sp", lt[:, i, :], I[nm].partition_broadcast(128), writes=[Buf()])
                    p.flush()
                    p.op("dve", lambda e: e.tensor_tensor(out=lt[:, 0, :], in0=lt[:, 0, :], in1=lt[:, 1, :], op=ALU.mult),
                         writes=[ltb])
                    p.op("dve", lambda e: e.tensor_tensor(out=lt[:, 2, :], in0=lt[:, 2, :], in1=lt[:, 3, :], op=ALU.mult),
                         writes=[ltb])
                    p.op("dve", lambda e: e.reduce_sum(out=ssum[:, 0:1], in_=lt[:, 0, :], axis=AX.X), reads=[ltb],
                         writes=[ssumb])
                    p.op("dve", lambda e: e.reduce_sum(out=ssum[:, 1:2], in_=lt[:, 2, :], axis=AX.X), reads=[ltb],
                         writes=[ssumb])
                    p.op("act", lambda e: e.activation(out=ssum[:, :], in_=ssum[:, :], func=AF.Exp), reads=[ssumb],
                         writes=[ssumb])
                    p.op("dve", lambda e: e.tensor_tensor(out=nlam[:, :], in0=ssum[:, 1:2], in1=ssum[:, 0:1],
                                                          op=ALU.subtract), reads=[ssumb], writes=[nlamb])
                    p.op("dve", lambda e: e.tensor_scalar(out=nlam[:, :], in0=nlam[:, :], scalar1=-lam_init, scalar2=None,
                                                          op0=ALU.add), reads=[nlamb], writes=[nlamb])
                    p.flush()
            qh = [self.sb(st, "qh", [128, S], BF16) for _ in range(2)]
            kh = [self.sb(st, "kh", [128, S], BF16) for _ in range(2)]
            vh = [self.sb(st, "vh", [128, NBLK, dv + 1], BF16) for _ in range(2)]
            qhb = [Buf("qh") for _ in range(2)]
            khb = [Buf("kh") for _ in range(2)]
            vhb = [Buf("vh") for _ in range(2)]
            PT = [self.sb(st, "PT", [128, NBLK, 512], BF16) for _ in range(2)]
            PTb = [[Buf("PT") for _ in range(NBLK)] for _ in range(2)]
            Yt = [self.sb(st, "Yt", [128, 4, dv], BF16) for _ in range(2)]
            Ytb = [Buf("Yt") for _ in range(2)]
            sm = [self.sb(st, "sm", [128, 8], F32) for _ in range(4)]
            smb = [Buf("sm") for _ in range(4)]
            if not even:
                O0 = [self.sb(st, "O0", [128, 4, 128], F32) for _ in range(2)]
                O0b = [Buf("O0") for _ in range(2)]
                Of = [self.sb(st, "Of", [128, 128], F32) for _ in range(2)]
                Ofb = [Buf("Of") for _ in range(2)]
                junk = self.sb(st, "junk", [128, 128], F32)
                junkb = Buf("junk", strict=True)

            if even:
                khm = [[kh[0]], [kh[1]]]
                for i_ in range(2):
                    p.op("pool", lambda e, i_=i_: e.memset(kh[i_][64:128, :], 0.0), writes=[khb[i_]])
                    p.op("pool", lambda e, i_=i_: e.memset(qh[i_][64:128, :], 0.0), writes=[qhb[i_]])
            else:
                kh2 = [self.sb(st, "kh2", [128, S], BF16) for _ in range(2)]
                khm = [[kh[0], kh2[0]], [kh[1], kh2[1]]]
                for i_ in range(2):
                    p.op("pool", lambda e, i_=i_: e.memset(kh[i_][64:128, :], 0.0), writes=[khb[i_]])
                    p.op("pool", lambda e, i_=i_: e.memset(kh2[i_][0:64, :], 0.0), writes=[khb[i_]])
            p.flush()

            def load(h):
                i = h % 2
                p.dma("sp", qh[i][0:qrows, :], T["qT"][h * qrows:(h + 1) * qrows, :], writes=[qhb[i]])
                p.dma("sp", kh[i][0:64, :], T["kT"][h * qrows:h * qrows + 64, :], writes=[khb[i]])
                if not even:
                    p.dma("sp", kh2[i][64:128, :], T["kT"][h * qrows + 64:h * qrows + 128, :], writes=[khb[i]])
                p.dma("sp", vh[i][:, :, :], vE[:, h, :].rearrange("(j p) c -> p j c", p=128), writes=[vhb[i]])

            cnt = {"kq": 0, "ko": 0, "ky": 0, "ksm": 0, "kO": 0}
            cur_y = [None, None]

            def gen_qk(h, qt, m, pt, ptb):
                i = h % 2
                for j in range(4 * qt + 4):
                    r = j - 4 * qt
                    q0 = max(r, 0) * 128
                    ps_, psb_ = pss[cnt["kq"] % 4], pssb[cnt["kq"] % 4]
                    cnt["kq"] += 1
                    p.op("pe", lambda e, ps_=ps_, i=i, m=m, j=j, q0=q0, qt=qt: e.matmul(
                        out=ps_[:, q0:512], lhsT=khm[i][m][:, j * 128:(j + 1) * 128],
                        rhs=qh[i][:, qt * TT + q0:(qt + 1) * TT], start=True, stop=True),
                        reads=[khb[i], qhb[i]], writes=[psb_])
                    if even:
                        p.op("act", lambda e, ps_=ps_, pt=pt, j=j, q0=q0, h=h, qt=qt: e.activation(
                            out=pt[:, j, q0:512], in_=ps_[:, q0:512], func=AF.Exp,
                            bias=biasAll[:, h, qt, j:j + 1]), reads=[psb_, biasb], writes=[ptb[j]])
                    else:
                        p.op("act", lambda e, ps_=ps_, pt=pt, j=j, q0=q0: e.activation(
                            out=pt[:, j, q0:512], in_=ps_[:, q0:512], func=AF.Exp),
                            reads=[psb_], writes=[ptb[j]])
                    if r >= 0:
                        p.op("pool", lambda e, pt=pt, j=j, q0=q0: e.tensor_tensor(
                            out=pt[:, j, q0:q0 + 128], in0=pt[:, j, q0:q0 + 128], in1=self.tri[:, :], op=ALU.mult),
                            reads=[ptb[j], self.trib], writes=[ptb[j]])
                    yield

            def gen_pv(h, qt, m, pt, ptb):
                i = h % 2
                if m == nmaps - 1:
                    cur_y[0], cur_y[1] = Yt[cnt["ky"] % 2], Ytb[cnt["ky"] % 2]
                    cnt["ky"] += 1
                yt, ytb = cur_y
                for qb in range(4):
                    po, pob = pos[cnt["ko"] % 4], posb[cnt["ko"] % 4]
                    cnt["ko"] += 1
                    nk = 4 * qt + qb + 1
                    for j in range(nk):
                        p.op("pe", lambda e, po=po, pt=pt, j=j, qb=qb, i=i, nk=nk: e.matmul(
                            out=po[:, 0:dv + 1], lhsT=pt[:, j, qb * 128:(qb + 1) * 128], rhs=vh[i][:, j, :],
                            start=(j == 0), stop=(j == nk - 1)), reads=[ptb[j], vhb[i]], writes=[pob])
                        if j % 4 == 3:
                            yield
                    s_, sb_ = sm[cnt["ksm"] % 4], smb[cnt["ksm"] % 4]
                    cnt["ksm"] += 1
                    p.op("dve", lambda e, po=po, s_=s_: e.reciprocal(out=s_[:, 0:1], in_=po[:, dv:dv + 1]),
                         reads=[pob], writes=[sb_])
                    if even:
                        p.op("dve", lambda e, po=po, s_=s_, yt=yt, qb=qb: e.tensor_scalar(
                            out=yt[:, qb, :], in0=po[:, 0:dv], scalar1=s_[:, 0:1], scalar2=None, op0=ALU.mult),
                            reads=[pob, sb_], writes=[ytb])
                    elif m == 0:
                        kO = cnt["kO"]
                        o0, o0b = O0[(kO // 4) % 2], O0b[(kO // 4) % 2]
                        p.op("dve", lambda e, po=po, s_=s_, o0=o0, qb=qb: e.tensor_scalar(
                            out=o0[:, qb, :], in0=po[:, 0:dv], scalar1=s_[:, 0:1], scalar2=None, op0=ALU.mult),
                            reads=[pob, sb_], writes=[o0b])
                    else:
                        kO = cnt["kO"]
                        o0, o0b = O0[(kO // 4) % 2], O0b[(kO // 4) % 2]
                        of_, ofb_ = Of[kO % 2], Ofb[kO % 2]
                        cnt["kO"] += 1
                        p.op("dve", lambda e, s_=s_: e.tensor_tensor(out=s_[:, 1:2], in0=s_[:, 0:1], in1=nlam[:, 0:1],
                                                                    op=ALU.mult), reads=[sb_, nlamb], writes=[sb_])
                        p.op("dve", lambda e, po=po, s_=s_, o0=o0, of_=of_, qb=qb: e.scalar_tensor_tensor(
                            out=of_[:, :], in0=po[:, 0:dv], scalar=s_[:, 1:2], in1=o0[:, qb, :],
                            op0=ALU.mult, op1=ALU.add), reads=[pob, sb_, o0b], writes=[ofb_])
                        p.op("dve", lambda e, of_=of_, s_=s_: e.scalar_tensor_tensor(
                            out=junk[:, :], in0=of_[:, :], scalar=1.0, in1=of_[:, :], op0=ALU.mult, op1=ALU.mult,
                            accum_out=s_[:, 2:3]), reads=[ofb_], writes=[junkb, sb_])
                        p.op("dve", lambda e, s_=s_: e.tensor_scalar(
                            out=s_[:, 3:4], in0=s_[:, 2:3], scalar1=1.0 / 128, scalar2=EPS, op0=ALU.mult,
                            op1=ALU.add), reads=[sb_], writes=[sb_])
                        p.op("pool", lambda e, s_=s_: e.tensor_tensor(out=s_[:, 5:6], in0=s_[:, 3:4], in1=mhalf[:, 0:1],
                                                                     op=ALU.pow), reads=[sb_, mhalfb], writes=[sb_])
                        p.op("dve", lambda e, s_=s_, of_=of_, yt=yt, qb=qb: e.tensor_scalar(
                            out=yt[:, qb, :], in0=of_[:, :], scalar1=s_[:, 5:6], scalar2=None, op0=ALU.mult),
                            reads=[ofb_, sb_], writes=[ytb])
                    yield
                if m == nmaps - 1:
                    c0 = (512 + h * 64) if even else h * 128
                    p.dma("sp", T["y"][qt * TT:(qt + 1) * TT, c0:c0 + dv].rearrange("(j p) d -> p j d", p=128),
                          yt[:, :, :], reads=[ytb])
                yield

            if not even:
                mhalf = self.sb(st, "mhalf", [128, 1], F32)
                mhalfb = Buf("mhalf")
                p.op("pool", lambda e: e.memset(mhalf[:, :], -0.5), writes=[mhalfb])
            steps = [(h, qt, m) for h in range(8) for qt in range(NTT) for m in range(nmaps)]
            load(0)
            load(1)

            def drain(g):
                for _ in g:
                    pass

            g0 = gen_qk(*steps[0], PT[0], PTb[0])
            drain(g0)
            wg_ = self.midw["gen"] if self.midw else None
            for s, (h, qt, m) in enumerate(steps):
                if qt == 0 and m == 0 and 1 <= h and h + 1 < 8:
                    load(h + 1)
                if wg_ is not None and s >= 2:
                    if next(wg_, "done") == "done":
                        wg_ = None
                gp = gen_pv(h, qt, m, PT[s % 2], PTb[s % 2])
                gq = None
                if s + 1 < len(steps):
                    gq = gen_qk(*steps[s + 1], PT[(s + 1) % 2], PTb[(s + 1) % 2])
                alive_p, alive_q = True, gq is not None
                for _lead in range(4):
                    if alive_q and next(gq, "done") == "done":
                        alive_q = False
                while alive_p or alive_q:
                    if alive_q:
                        try:
                            next(gq)
                        except StopIteration:
                            alive_q = False
                    if alive_p:
                        try:
                            next(gp)
                        except StopIteration:
                            alive_p = False
            if wg_ is not None:
                for _ in wg_:
                    pass
            p.flush()

    def midw_alloc(self, st, layer):
        p, I = self.p, self.I
        even = (layer % 2 == 0)
        lam_init = 0.8 - 0.6 * math.exp(-0.3 * layer)
        M = {}
        names = ("wout", "wq", "wo") if even else ("wout", "wq")
        for n in names:
            M[n] = self.sb(st, n, [128, 8, D], BF16)
            M[n + "b"] = Buf(n)
        gcol, gb = self.load_col(st, "gx", I["xattn_norm_g"][layer], 8)
        M["gcol"], M["gb"] = gcol, gb
        items = []
        if even:
            for kc in range(8):
                items.append((M["wout"][:, kc, :], M["woutb"], I["w_out_even"][kc * 128:(kc + 1) * 128, :], None, None, 1.0))
        else:
            gd = self.sb(st, "gd", [128, 8], F32)
            gdb = Buf("gd")
            for kc in range(8):
                p.dma("sp", gd[:, kc:kc + 1], I["diff_norm_g"].rearrange("(p o) -> p o", o=1), writes=[gdb])
            for kc in range(8):
                items.append((M["wout"][:, kc, :], M["woutb"], I["w_out_odd"][kc * 128:(kc + 1) * 128, :],
                              gd[:, kc:kc + 1], gdb, 1.0 - lam_init))
        for kc in range(8):
            items.append((M["wq"][:, kc, :], M["wqb"], I["xattn_wq"][layer][kc * 128:(kc + 1) * 128, :],
                          gcol[:, kc:kc + 1], gb, 1.0))
        if even:
            for kc in range(8):
                items.append((M["wo"][:, kc, :], M["wob"], I["xattn_wo"][layer][kc * 128:(kc + 1) * 128, :], None, None, 1.0))
        NS = 3
        stgs = [self.sb(st, "pstg", [128, D], F32) for _ in range(NS)]
        stgb = [Buf("pstg") for _ in range(NS)]

        def gen():
            pend = []

            def cast(it, stg, sbf):
                dst, dstb, _, g, gbuf, mul2 = it
                if g is not None:
                    p.op("pool", lambda e: e.tensor_scalar(out=dst, in0=stg[:, :], scalar1=g, scalar2=float(mul2),
                                                           op0=ALU.mult, op1=ALU.mult), reads=[sbf, gbuf], writes=[dstb])
                else:
                    p.op("pool", lambda e: e.tensor_copy(out=dst, in_=stg[:, :]), reads=[sbf], writes=[dstb])

            for idx, it in enumerate(items):
                stg, sbf = stgs[idx % NS], stgb[idx % NS]
                p.dma("sp", stg[:, :], it[2], writes=[sbf])
                pend.append((it, stg, sbf))
                if len(pend) > NS - 1:
                    cast(*pend.pop(0))
                yield
            while pend:
                cast(*pend.pop(0))
                yield

        M["gen"] = gen()
        return M

    def phase_mid(self, layer):
        p, I, T, Tb = self.p, self.I, self.T, self.Tb
        even = (layer % 2 == 0)
        lam_init = 0.8 - 0.6 * math.exp(-0.3 * layer)
        hsrc = I["x"] if layer == 0 else T["h"]
        with ExitStack() as st:
            M = self.midw
            wout, woutb, wq, wqb = M["wout"], M["woutb"], M["wq"], M["wqb"]
            if "wo" in M:
                wo, wob = M["wo"], M["wob"]
            else:
                wo = self.sb(st, "wo", [128, 8, D], BF16)
                wob = Buf("wo")
                self.load_w(st, wo, wob, I["xattn_wo"][layer], 8, D, ncol_chunk=1024)
            kT, kTb = self.kmemT[layer], self.kmemTb[layer]
            vE, vEb = self.vmemE[layer], self.vmemEb[layer]
            xs = [self.sb(st, "x", [128, 4, D], F32) for _ in range(2)]
            xsb = [Buf("x") for _ in range(2)]
            yin = [self.sb(st, "yin", [128, 4, D], BF16) for _ in range(2)]
            yinb = [Buf("yin") for _ in range(2)]
            YT = [self.sb(st, "YT", [128, 8, TT], BF16) for _ in range(2)]
            YTb = [Buf("YT") for _ in range(2)]
            YTlb = [Buf("YTl") for _ in range(2)]
            xn = self.sb(st, "xn", [128, 4, D], BF16)
            xnb = Buf("xn")
            xT = self.sb(st, "xT", [128, 8, TT], BF16)
            xTb = Buf("xT")
            qT = self.sb(st, "qT", [128, 8, TT], BF16)
            qTb = Buf("qT")
            PTm = self.sb(st, "PTm", [128, 4, 2, TT], BF16)
            PTmb = [[Buf("PTm") for _ in range(2)] for _ in range(4)]
            junk = self.sb(st, "junk", [128, D], BF16)
            junkb = Buf("junk", strict=True)
            ss = self.sb(st, "ss", [128, 4], F32)
            ssb = Buf("ss")
            rstd = self.sb(st, "rstd", [128, 4], F32)
            rstdb = Buf("rstd")
            sm = [self.sb(st, "sm", [128, 2], F32) for _ in range(4)]
            smb = [Buf("sm") for _ in range(4)]
            pst = [self.ps(st, "pst", [128, 8, 128], BF16) for _ in range(2)]
            pstb = [Buf("pst") for _ in range(2)]
            pso = [self.ps(st, "pso", [128, 512], F32) for _ in range(4)]
            psob = [Buf("pso") for _ in range(4)]
            pos = [self.ps(st, "pos", [128, 512], F32) for _ in range(2)]
            posb = [Buf("pos") for _ in range(2)]

            def load(tt):
                i = tt % 2
                p.dma("sp", xs[i][:, :, :], hsrc[tt * TT:(tt + 1) * TT, :].rearrange("(j p) d -> p j d", p=128),
                      writes=[xsb[i]])
                if even:
                    p.dma("sp", YT[i][:, 0:4, :], T["ylruT"][:, tt * TT:(tt + 1) * TT].rearrange("(c p) t -> p c t", p=128),
                          writes=[YTlb[i]])
                    p.dma("sp", yin[i][:, :, 0:512],
                          T["y"][tt * TT:(tt + 1) * TT, 512:1024].rearrange("(j p) d -> p j d", p=128), writes=[yinb[i]])
                else:
                    p.dma("sp", yin[i][:, :, :], T["y"][tt * TT:(tt + 1) * TT, :].rearrange("(j p) d -> p j d", p=128),
                          writes=[yinb[i]])

            k = 0
            ko = 0
            ksm = 0
            ev = 0
            load(0)
            for tt in range(NTT):
                if tt + 1 < NTT:
                    load(tt + 1)
                i = tt % 2
                x, xb = xs[i], xsb[i]
                if even:
                    self.transpose_to(yin[i], yinb[i], YT[i][:, 4:8, :], YTb[i], 4, 4, pst, pstb, self.ident, self.identb)
                    ytreads = [YTb[i], YTlb[i]]
                else:
                    self.transpose_to(yin[i], yinb[i], YT[i], YTb[i], 4, 8, pst, pstb, self.ident, self.identb)
                    ytreads = [YTb[i]]

                def proj_add(srcT, srcreads, w, wb_):
                    nonlocal k
                    for j in range(4):
                        for ct in range(2):
                            po, pob = pso[k % 4], psob[k % 4]
                            k += 1
                            for kc in range(8):
                                p.op("pe", lambda e, po=po, kc=kc, j=j, ct=ct: e.matmul(
                                    out=po[:, :], lhsT=srcT[:, kc, j * 128:(j + 1) * 128],
                                    rhs=w[:, kc, ct * 512:(ct + 1) * 512], start=(kc == 0), stop=(kc == 7)),
                                    reads=srcreads + [wb_], writes=[pob])
                            p.op("dve", lambda e, po=po, j=j, ct=ct, x=x: e.tensor_tensor(
                                out=x[:, j, ct * 512:(ct + 1) * 512], in0=po[:, :], in1=x[:, j, ct * 512:(ct + 1) * 512],
                                op=ALU.add), reads=[pob, xb], writes=[xb])

                proj_add(YT[i], ytreads, wout, woutb)
                if "hmix" in self.debug:
                    if tt == 0:
                        self.T["hmix"] = self.dram_tmp("hmix", [S, D], F32)
                        self.T["xo"] = self.dram_tmp("xo", [S, D], BF16)
                    p.dma("sp", self.T["hmix"][tt * TT:(tt + 1) * TT, :].rearrange("(j p) d -> p j d", p=128), x[:, :, :], reads=[xb])
                self.rms_cast(x, xb, xn, xnb, 4, junk, junkb, ss, ssb, rstd, rstdb)
                self.transpose_to(xn, xnb, xT, xTb, 4, 8, pst, pstb, self.ident, self.identb)
                for ct in range(8):
                    po, pob = pso[k % 4], psob[k % 4]
                    k += 1
                    for kc in range(8):
                        p.op("pe", lambda e, po=po, kc=kc, ct=ct: e.matmul(
                            out=po[:, :], lhsT=wq[:, kc, ct * 128:(ct + 1) * 128], rhs=xT[:, kc, :],
                            start=(kc == 0), stop=(kc == 7)), reads=[wqb, xTb], writes=[pob])
                    if ev % 2 == 0:
                        p.op("act", lambda e, po=po, ct=ct: e.activation(out=qT[:, ct, :], in_=po[:, :], func=AF.Copy,
                                                                         scale=1.0 / 16), reads=[pob], writes=[qTb])
                    else:
                        p.op("dve", lambda e, po=po, ct=ct: e.tensor_scalar(out=qT[:, ct, :], in0=po[:, :], scalar1=1.0 / 16,
                                                                            scalar2=None, op0=ALU.mult),
                             reads=[pob], writes=[qTb])
                    ev += 1
                for hh in range(4):
                    for mc in range(2):
                        po, pob = pso[k % 4], psob[k % 4]
                        k += 1
                        for dc in range(2):
                            p.op("pe", lambda e, po=po, hh=hh, mc=mc, dc=dc: e.matmul(
                                out=po[:, :], lhsT=kT[:, hh * 2 + dc, mc * 128:(mc + 1) * 128], rhs=qT[:, hh * 2 + dc, :],
                                start=(dc == 0), stop=(dc == 1)), reads=[kTb, qTb], writes=[pob])
                        p.op("act", lambda e, po=po, hh=hh, mc=mc: e.activation(out=PTm[:, hh, mc, :], in_=po[:, :],
                                                                                func=AF.Exp),
                             reads=[pob], writes=[PTmb[hh][mc]])
                    for j in range(4):
                        po, pob = pos[ko % 2], posb[ko % 2]
                        ko += 1
                        for mc in range(2):
                            p.op("pe", lambda e, po=po, hh=hh, mc=mc, j=j: e.matmul(
                                out=po[:, 0:257], lhsT=PTm[:, hh, mc, j * 128:(j + 1) * 128], rhs=vE[:, mc, hh, :],
                                start=(mc == 0), stop=(mc == 1)), reads=[PTmb[hh][mc], vEb], writes=[pob])
                        s_, sb_ = sm[ksm % 4], smb[ksm % 4]
                        ksm += 1
                        p.op("dve", lambda e, po=po, s_=s_: e.reciprocal(out=s_[:, 0:1], in_=po[:, 256:257]),
                             reads=[pob], writes=[sb_])
                        if ksm % 2 == 0:
                            p.op("act", lambda e, po=po, s_=s_, j=j, hh=hh: e.activation(
                                out=xn[:, j, hh * 256:(hh + 1) * 256], in_=po[:, 0:256], func=AF.Copy, scale=s_[:, 0:1]),
                                reads=[pob, sb_, xTb], writes=[xnb])
                        else:
                            p.op("dve", lambda e, po=po, s_=s_, j=j, hh=hh: e.tensor_scalar(
                                out=xn[:, j, hh * 256:(hh + 1) * 256], in0=po[:, 0:256], scalar1=s_[:, 0:1], scalar2=None,
                                op0=ALU.mult), reads=[pob, sb_, xTb], writes=[xnb])
                if "hmix" in self.debug:
                    p.dma("sp", self.T["xo"][tt * TT:(tt + 1) * TT, :].rearrange("(j p) d -> p j d", p=128), xn[:, :, :], reads=[xnb])
                self.transpose_to(xn, xnb, qT, qTb, 4, 8, pst, pstb, self.ident, self.identb)
                proj_add(qT, [qTb], wo, wob)
                p.dma("sp", T["h"][tt * TT:(tt + 1) * TT, :].rearrange("(j p) d -> p j d", p=128), x[:, :, :], reads=[xb])
            p.flush()

    def phase_mlp(self, layer):
        p, I, T, Tb = self.p, self.I, self.T, self.Tb
        last = (layer == DEPTH - 1)
        MT = 512
        NMT = S // MT
        HF = DFF // 2
        NFC = HF // 128
        with ExitStack() as st:
            if self.WA is not None:
                wup0 = self.WA[:, 0:16384].rearrange("p (k n) -> p k n", k=8)
                wdn0 = self.WA[:, 16384:32768].rearrange("p (k n) -> p k n", k=NFC)
            else:
                wup0 = self.sb(st, "wup", [128, 8, HF], BF16)
                wdn0 = self.sb(st, "wdn", [128, NFC, D], BF16)
            wup = [wup0, self.sb(st, "wup", [128, 8, HF], BF16)]
            wupb = [Buf("wup") for _ in range(2)]
            wdn = [wdn0, self.sb(st, "wdn", [128, NFC, D], BF16)]
            wdnb = [Buf("wdn") for _ in range(2)]
            gcol, gb = self.load_col(st, "gmlp", I["mlp_norm_g"][layer], 8)
            if self.WA is not None:
                gnx, gnxb = self.load_col(st, "gmixn", I["mix_norm_g"][layer + 1], 8)
            self._stg = [self.sb(st, "stg", [128, 1024], F32) for _ in range(2)]
            self._stgb = [Buf("stg%d" % i) for i in range(2)]
            self._stg_st = st

            def wgen(hf, engs=None):
                yield from self.load_w_gen(st, wup[hf], wupb[hf], I["w_up"][layer][:, hf * HF:(hf + 1) * HF], 8, HF,
                                           gcol=gcol, gbuf=gb, ncol_chunk=1024, engs=engs)
                yield from self.load_w_gen(st, wdn[hf], wdnb[hf], I["w_down"][layer][hf * HF:(hf + 1) * HF, :], NFC, D,
                                           ncol_chunk=1024, engs=engs)

            for _ in wgen(0):
                pass
            if last:
                gfin = self.sb(st, "gfin", [128, D], F32)
                gfinb = Buf("gfin")
                p.dma("sp", gfin[:, :], I["g_final"].partition_broadcast(128), writes=[gfinb])
            xh = [self.sb(st, "xh", [128, 2, D], F32) for _ in range(2)]
            xhb = [Buf("xh") for _ in range(2)]
            big = self.sb(st, "big", [128, NFC * MT], BF16)
            bigb = Buf("big")
            xn = big[:, 0:4 * D].rearrange("p (j d) -> p j d", j=4)
            hT = big[:, :].rearrange("p (f t) -> p f t", f=NFC)
            xT = self.sb(st, "xT", [128, 8, MT], BF16)
            xTb = Buf("xT")
            rl = [self.sb(st, "rl", [128, MT], F32) for _ in range(2)]
            rlb = [Buf("rl") for _ in range(2)]
            junk = self.sb(st, "junk", [128, D], BF16)
            junkb = Buf("junk", strict=True)
            ss = self.sb(st, "ss", [128, 4], F32)
            ssb = Buf("ss")
            rstd = self.sb(st, "rstd", [128, 4], F32)
            rstdb = Buf("rstd")
            pst = [self.ps(st, "pst", [128, 8, 128], BF16) for _ in range(2)]
            pstb = [Buf("pst") for _ in range(2)]
            pso = [self.ps(st, "pso", [128, 512], F32) for _ in range(6)]
            psob = [Buf("pso") for _ in range(6)]
            xnT = T["xnT"]

            def load_x(t, half):
                r0 = t * MT + half * 256
                p.dma("act", xh[half][:, :, :], T["h"][r0:r0 + 256, :].rearrange("(j p) d -> p j d", p=128),
                      writes=[xhb[half]])

            def load_xT(t):
                p.dma("sp", xT[:, :, :], xnT[:, t * MT:(t + 1) * MT].rearrange("(c p) t -> p c t", p=128), writes=[xTb])

            k = 0
            kr = 0
            for hf in range(2):
                pre = wgen(1, engs=("pool", "dve")) if hf == 0 else None
                if hf == 1 and self.WA is not None:
                    nodd = ((layer + 1) % 2 == 1)
                    NIN_ = 3 * D if nodd else EVEN_IN
                    self.win_pre = self.WA[:, 0:8 * NIN_].rearrange("p (k n) -> p k n", k=8)
                    pre = self.load_w_gen(st, self.win_pre, [wupb[0], wdnb[0]],
                                          I["w_in_odd"] if nodd else I["w_in_even"], 8, NIN_, gcol=gnx, gbuf=gnxb,
                                          ncol_chunk=1024, engs=("pool", "dve"))
                load_x(0, 0)
                load_x(0, 1)
                if hf == 1:
                    load_xT(0)
                for t in range(NMT):
                    if hf == 0:
                        for half in range(2):
                            x, xb = xh[half], xhb[half]
                            for jj in range(2):
                                p.op("act", lambda e, x=x, jj=jj, half=half: e.activation(
                                    out=xn[:, 2 * half + jj, :], in_=x[:, jj, :], func=AF.Square,
                                    accum_out=ss[:, 2 * half + jj:2 * half + jj + 1]), reads=[xb], writes=[bigb, ssb])
                        p.op("dve", lambda e: e.tensor_scalar(out=rstd[:, :], in0=ss[:, :], scalar1=1.0 / D, scalar2=EPS,
                                                              op0=ALU.mult, op1=ALU.add), reads=[ssb], writes=[rstdb])
                        p.op("act", lambda e: e.activation(out=rstd[:, :], in_=rstd[:, :], func=AF.Sqrt),
                             reads=[rstdb], writes=[rstdb])
                        p.op("dve", lambda e: e.reciprocal(out=rstd[:, :], in_=rstd[:, :]), reads=[rstdb], writes=[rstdb])
                        for half in range(2):
                            x, xb = xh[half], xhb[half]
                            for jj in range(2):
                                j = 2 * half + jj
                                p.op("dve", lambda e, x=x, jj=jj, j=j: e.tensor_scalar(
                                    out=xn[:, j, :], in0=x[:, jj, :], scalar1=rstd[:, j:j + 1], scalar2=None,
                                    op0=ALU.mult), reads=[xb, rstdb], writes=[bigb])
                        self.transpose_to(xn, bigb, xT, xTb, 4, 8, pst, pstb, self.ident, self.identb)
                        p.dma("sp", xnT[:, t * MT:(t + 1) * MT].rearrange("(c p) t -> p c t", p=128), xT[:, :, :],
                              reads=[xTb])
                    for fc in range(NFC):
                        po, pob = pso[k % 6], psob[k % 6]
                        k += 1
                        for kc in range(8):
                            p.op("pe", lambda e, po=po, kc=kc, fc=fc, hf=hf: e.matmul(
                                out=po[:, :], lhsT=wup[hf][:, kc, fc * 128:(fc + 1) * 128], rhs=xT[:, kc, :],
                                start=(kc == 0), stop=(kc == 7)), reads=[wupb[hf], xTb], writes=[pob])
                        r_, rb_ = rl[kr % 2], rlb[kr % 2]
                        kr += 1
                        p.op("act", lambda e, po=po, r_=r_: e.activation(out=r_[:, :], in_=po[:, :], func=AF.Relu),
                             reads=[pob], writes=[rb_])
                        eng = "dve" if fc % 2 == 0 else "pool"
                        p.op(eng, lambda e, r_=r_, fc=fc: e.tensor_tensor(out=hT[:, fc, :], in0=r_[:, :], in1=r_[:, :],
                                                                          op=ALU.mult), reads=[rb_], writes=[bigb])
                        if pre is not None and fc % 4 == 3:
                            next(pre, None)
                    if hf == 1 and t + 1 < NMT:
                        load_xT(t + 1)
                    for half in range(2):
                        x, xb = xh[half], xhb[half]
                        for jj in range(2):
                            j = 2 * half + jj
                            pos_ = []
                            for ct in range(2):
                                pos_.append((pso[k % 6], psob[k % 6]))
                                k += 1
                            for fc in range(NFC):
                                for ct in range(2):
                                    po, pob = pos_[ct]
                                    p.op("pe", lambda e, po=po, fc=fc, j=j, ct=ct, hf=hf: e.matmul(
                                        out=po[:, :], lhsT=hT[:, fc, j * 128:(j + 1) * 128],
                                        rhs=wdn[hf][:, fc, ct * 512:(ct + 1) * 512],
                                        start=(fc == 0), stop=(fc == NFC - 1)), reads=[bigb, wdnb[hf]], writes=[pob])
                            for ct in range(2):
                                po, pob = pos_[ct]
                                p.op("dve", lambda e, po=po, jj=jj, ct=ct, x=x: e.tensor_tensor(
                                    out=x[:, jj, ct * 512:(ct + 1) * 512], in0=po[:, :],
                                    in1=x[:, jj, ct * 512:(ct + 1) * 512], op=ALU.add), reads=[pob, xb], writes=[xb])
                        r0 = t * MT + half * 256
                        if not (last and hf == 1):
                            p.dma("sp", T["h"][r0:r0 + 256, :].rearrange("(j p) d -> p j d", p=128), x[:, :, :],
                                  reads=[xb])
                        else:
                            for jj in range(2):
                                p.op("act", lambda e, jj=jj, x=x, half=half: e.activation(
                                    out=junk[:, :], in_=x[:, jj, :], func=AF.Square,
                                    accum_out=ss[:, 2 * half + jj:2 * half + jj + 1]), reads=[xb], writes=[junkb, ssb])
                            c0 = 2 * half
                            p.op("dve", lambda e, c0=c0: e.tensor_scalar(
                                out=rstd[:, c0:c0 + 2], in0=ss[:, c0:c0 + 2], scalar1=1.0 / D, scalar2=EPS, op0=ALU.mult,
                                op1=ALU.add), reads=[ssb], writes=[rstdb])
                            p.op("act", lambda e, c0=c0: e.activation(out=rstd[:, c0:c0 + 2], in_=rstd[:, c0:c0 + 2],
                                                                      func=AF.Sqrt), reads=[rstdb], writes=[rstdb])
                            p.op("dve", lambda e, c0=c0: e.reciprocal(out=rstd[:, c0:c0 + 2], in_=rstd[:, c0:c0 + 2]),
                                 reads=[rstdb], writes=[rstdb])
                            for jj in range(2):
                                p.op("dve", lambda e, jj=jj, x=x, c0=c0: e.scalar_tensor_tensor(
                                    out=x[:, jj, :], in0=x[:, jj, :], scalar=rstd[:, c0 + jj:c0 + jj + 1], in1=gfin[:, :],
                                    op0=ALU.mult, op1=ALU.mult), reads=[xb, rstdb, gfinb], writes=[xb])
                            p.dma("sp", self.out[r0:r0 + 256, :].rearrange("(j p) d -> p j d", p=128), x[:, :, :],
                                  reads=[xb])
                        if t + 1 < NMT:
                            load_x(t + 1, half)
                if pre is not None:
                    for _ in pre:
                        pass
            p.flush()


def make_in_maps(inputs):
    f = lambda a: np.ascontiguousarray(np.asarray(a, dtype=np.float32))
    shared = {}
    for n in ("g_mem", "g_final", "mix_norm_g", "xattn_norm_g", "mlp_norm_g", "xattn_wq", "xattn_wkv",
              "xattn_wo", "w_up", "w_down"):
        shared[n] = f(inputs[n])
    for n in ("w_in_even", "conv_w", "conv_b", "w_rgate", "b_rgate", "w_igate", "b_igate", "lru_lambda",
              "fox_forget_b", "w_out_even", "w_in_odd", "lambda_q1", "lambda_k1", "lambda_q2", "lambda_k2",
              "diff_norm_g", "w_out_odd"):
        shared[n] = f(np.asarray(inputs[n])[0])
    zeros = {k: np.zeros_like(v) for k, v in shared.items()}
    zx = np.zeros((S, D), np.float32)
    zm = np.zeros((NMEM, D), np.float32)
    maps = []
    for c in range(N_CORES):
        if c in CORE_OF_BATCH:
            b = CORE_OF_BATCH.index(c)
            m = dict(shared)
            m["x"] = f(np.asarray(inputs["x"])[b])
            m["mem"] = f(np.asarray(inputs["mem"])[b])
        else:
            m = dict(zeros)
            m["x"] = zx
            m["mem"] = zm
        maps.append(m)
    return maps


_NC_CACHE = {}


def kernel(**inputs):
    if "nc" not in _NC_CACHE:
        _NC_CACHE["nc"] = Builder().build()
    nc = _NC_CACHE["nc"]
    maps = make_in_maps(inputs)
    res = run_bass_kernel_spmd(nc, maps, core_ids=list(range(N_CORES)))
    out = np.stack([np.asarray(res.results[CORE_OF_BATCH[b]]["out"], dtype=np.float32) for b in range(NB)], axis=0)
    return out
```
